# Optimizing a Trainium2 kernel written in Bass

```python
import math
import jax, jax.numpy as jnp
from jax import lax
import numpy as np

D_MODEL = 2048
BATCH = 4
SEQ = 2048
DEPTH = 1

DIFF_HEADS = 8
DIFF_QK_DIM = 64
DIFF_V_DIM = 128
DIFF_WIDTH = DIFF_HEADS * DIFF_V_DIM
SSD_HEADS = 8
SSD_HEAD_DIM = 64
SSD_WIDTH = SSD_HEADS * SSD_HEAD_DIM
SSD_GROUPS = 2
SSD_STATE = 128
SSD_CONV = 4
SSD_CHUNK = 128
SSD_CONV_DIM = SSD_WIDTH + 2 * SSD_GROUPS * SSD_STATE
XATTN_HEADS = 4
XATTN_HEAD_DIM = 128
XATTN_WIDTH = XATTN_HEADS * XATTN_HEAD_DIM
MEM_LEN = 256
D_MIX = DIFF_WIDTH + SSD_WIDTH + XATTN_WIDTH
ROPE_THETA = 10000.0
Q_BLOCK = 128
NORM_EPS = 1e-6

IN_SPLITS = (
    2 * DIFF_HEADS * DIFF_QK_DIM,
    2 * DIFF_HEADS * DIFF_QK_DIM,
    DIFF_WIDTH,
    DIFF_WIDTH,
    SSD_WIDTH,
    SSD_CONV_DIM,
    SSD_HEADS,
    XATTN_WIDTH,
    XATTN_WIDTH,
)
D_IN = sum(IN_SPLITS)

kernel_name = 'hymba_diffattn_ssd_memxattn_layer'


def rms_norm(x, w):
    xf = x.astype(jnp.float32)
    y = xf * lax.rsqrt(jnp.mean(xf * xf, axis=-1, keepdims=True) + NORM_EPS)
    return (y * w.astype(jnp.float32)).astype(x.dtype)


def rope_cos_sin(positions, dim):
    inv_freq = 1.0 / (ROPE_THETA ** (jnp.arange(0, dim, 2, dtype=jnp.float32) / dim))
    ang = positions.astype(jnp.float32)[..., None] * inv_freq
    ang = jnp.concatenate([ang, ang], axis=-1)
    return jnp.cos(ang), jnp.sin(ang)


def apply_rope(x, cos, sin):
    xf = x.astype(jnp.float32)
    x1, x2 = jnp.split(xf, 2, axis=-1)
    rot = jnp.concatenate([-x2, x1], axis=-1)
    c = cos[:, :, None, None, :]
    s = sin[:, :, None, None, :]
    return (xf * c + rot * s).astype(x.dtype)


def diff_attention(q, k, v, lam):
    B, H, _, S, Dk = q.shape
    Dv = v.shape[-1]
    nq = S // Q_BLOCK
    q_blocks = q.reshape(B, H, 2, nq, Q_BLOCK, Dk).transpose(3, 0, 1, 2, 4, 5)
    starts = jnp.arange(nq, dtype=jnp.int32) * Q_BLOCK
    key_pos = jnp.arange(S, dtype=jnp.int32)
    scale = Dk ** -0.5

    def one_block(args):
        q_blk, start = args
        s = jnp.einsum('bhcqd,bhckd->bhcqk', q_blk, k).astype(jnp.float32) * scale
        q_pos = start + jnp.arange(Q_BLOCK, dtype=jnp.int32)
        causal = key_pos[None, :] <= q_pos[:, None]
        s = jnp.where(causal, s, -jnp.inf)
        p = jax.nn.softmax(s, axis=-1)
        a = p[:, :, 0] - lam * p[:, :, 1]
        return jnp.einsum('bhqk,bhkv->bhqv', a.astype(v.dtype), v)

    out = lax.map(one_block, (q_blocks, starts))
    return out.transpose(1, 0, 3, 2, 4).reshape(B, S, H, Dv)


def causal_depthwise_conv(x, w, b):
    y = lax.conv_general_dilated(
        x, w[:, None, :].astype(x.dtype), window_strides=(1,),
        padding=[(SSD_CONV - 1, 0)], dimension_numbers=('NWC', 'WIO', 'NWC'),
        feature_group_count=x.shape[-1])
    return y + b.astype(x.dtype)


def ssd_chunked(xh, dt, A, Bg, Cg):
    Bsz, S, H, P = xh.shape
    N = Bg.shape[-1]
    rep = H // Bg.shape[2]
    L = SSD_CHUNK
    nc = S // L
    Bh = jnp.repeat(Bg.astype(jnp.float32), rep, axis=2).reshape(Bsz, nc, L, H, N)
    Ch = jnp.repeat(Cg.astype(jnp.float32), rep, axis=2).reshape(Bsz, nc, L, H, N)
    X = (xh.astype(jnp.float32) * dt[..., None]).reshape(Bsz, nc, L, H, P)
    a_dt = (dt * A).reshape(Bsz, nc, L, H).transpose(0, 3, 1, 2)
    a_cs = jnp.cumsum(a_dt, axis=-1)
    tri = jnp.tril(jnp.ones((L, L), dtype=bool))
    seg = a_cs[..., :, None] - a_cs[..., None, :]
    decay = jnp.exp(jnp.where(tri, seg, -jnp.inf))
    scores = jnp.einsum('bclhn,bcshn->bhcls', Ch, Bh) * decay
    y_diag = jnp.einsum('bhcls,bcshp->bclhp', scores, X)
    decay_to_end = jnp.exp(a_cs[..., -1:] - a_cs).transpose(0, 2, 3, 1)
    chunk_states = jnp.einsum('bclhn,bclhp->bchpn', Bh * decay_to_end[..., None], X)
    chunk_decay = jnp.exp(a_cs[..., -1])

    def step(state, inp):
        cs, dec = inp
        return state * dec[:, :, None, None] + cs, state

    init = jnp.zeros((Bsz, H, P, N), jnp.float32)
    _, prev_states = lax.scan(step, init, (chunk_states.transpose(1, 0, 2, 3, 4),
                                           chunk_decay.transpose(2, 0, 1)))
    prev_states = prev_states.transpose(1, 0, 2, 3, 4)
    decay_in = jnp.exp(a_cs).transpose(0, 2, 3, 1)
    y_off = jnp.einsum('bclhn,bchpn->bclhp', Ch, prev_states) * decay_in[..., None]
    return (y_diag + y_off).reshape(Bsz, S, H, P)


def hybrid_layer(x, mem, cos, sin, lambda_init, pre_norm_w, w_in, lambda_q1, lambda_k1,
                 lambda_q2, lambda_k2, diff_subln_w, conv_w, conv_b, dt_bias, a_log,
                 d_skip, ssd_norm_w, mem_norm_w, w_mem_kv, w_out, post_norm_w):
    B, S, _ = x.shape
    h = rms_norm(x, pre_norm_w)
    proj = h @ w_in
    offsets = [int(o) for o in np.cumsum(IN_SPLITS)[:-1]]
    dq, dk, dv, dg, z, xbc, dt_raw, xq, xg = jnp.split(proj, offsets, axis=-1)

    dq = apply_rope(dq.reshape(B, S, DIFF_HEADS, 2, DIFF_QK_DIM), cos, sin).transpose(0, 2, 3, 1, 4)
    dk = apply_rope(dk.reshape(B, S, DIFF_HEADS, 2, DIFF_QK_DIM), cos, sin).transpose(0, 2, 3, 1, 4)
    dv = dv.reshape(B, S, DIFF_HEADS, DIFF_V_DIM).transpose(0, 2, 1, 3)
    lam = (jnp.exp(jnp.sum(lambda_q1.astype(jnp.float32) * lambda_k1.astype(jnp.float32)))
           - jnp.exp(jnp.sum(lambda_q2.astype(jnp.float32) * lambda_k2.astype(jnp.float32)))
           + lambda_init)
    o = diff_attention(dq, dk, dv, lam)
    o = rms_norm(o, diff_subln_w) * (1.0 - lambda_init)
    diff_out = o.reshape(B, S, DIFF_WIDTH) * jax.nn.silu(dg)

    xbc = jax.nn.silu(causal_depthwise_conv(xbc, conv_w, conv_b))
    xs, bm, cm = jnp.split(xbc, [SSD_WIDTH, SSD_WIDTH + SSD_GROUPS * SSD_STATE], axis=-1)
    dt = jax.nn.softplus(dt_raw.astype(jnp.float32) + dt_bias.astype(jnp.float32))
    A = -jnp.exp(a_log.astype(jnp.float32))
    xh = xs.reshape(B, S, SSD_HEADS, SSD_HEAD_DIM)
    y = ssd_chunked(xh, dt, A, bm.reshape(B, S, SSD_GROUPS, SSD_STATE),
                    cm.reshape(B, S, SSD_GROUPS, SSD_STATE))
    y = y + xh.astype(jnp.float32) * d_skip.astype(jnp.float32)[:, None]
    y = y.reshape(B, S, SSD_WIDTH).astype(x.dtype) * jax.nn.silu(z)
    grp = SSD_WIDTH // SSD_GROUPS
    ssd_out = rms_norm(y.reshape(B, S, SSD_GROUPS, grp),
                       ssd_norm_w.reshape(SSD_GROUPS, grp)).reshape(B, S, SSD_WIDTH)

    mem_n = rms_norm(mem, mem_norm_w)
    mk, mv = jnp.split(mem_n @ w_mem_kv, 2, axis=-1)
    mk = mk.reshape(B, -1, XATTN_HEADS, XATTN_HEAD_DIM)
    mv = mv.reshape(B, -1, XATTN_HEADS, XATTN_HEAD_DIM)
    q = xq.reshape(B, S, XATTN_HEADS, XATTN_HEAD_DIM)
    s = jnp.einsum('bshd,bmhd->bhsm', q, mk).astype(jnp.float32) * (XATTN_HEAD_DIM ** -0.5)
    p = jax.nn.softmax(s, axis=-1)
    xo = jnp.einsum('bhsm,bmhd->bshd', p.astype(mv.dtype), mv).reshape(B, S, XATTN_WIDTH)
    xattn_out = xo * jax.nn.silu(xg)

    mixed = jnp.concatenate([diff_out, ssd_out, xattn_out], axis=-1)
    return x + rms_norm(mixed @ w_out, post_norm_w)


def setup_inputs(seed: int = 0) -> dict:
    key = jax.random.key(seed)
    ks = jax.random.split(key, 20)
    f32 = jnp.float32
    x = jax.random.normal(ks[0], (BATCH, SEQ, D_MODEL), f32)
    mem = jax.random.normal(ks[1], (BATCH, MEM_LEN, D_MODEL), f32)
    positions = jnp.broadcast_to(jnp.arange(SEQ, dtype=jnp.int32), (BATCH, SEQ))
    pre_norm_w = 1.0 + 0.02 * jax.random.normal(ks[2], (DEPTH, D_MODEL), f32)
    w_in = jax.random.normal(ks[3], (DEPTH, D_MODEL, D_IN), f32) * D_MODEL ** -0.5
    lambda_q1 = 0.1 * jax.random.normal(ks[4], (DEPTH, DIFF_QK_DIM), f32)
    lambda_k1 = 0.1 * jax.random.normal(ks[5], (DEPTH, DIFF_QK_DIM), f32)
    lambda_q2 = 0.1 * jax.random.normal(ks[6], (DEPTH, DIFF_QK_DIM), f32)
    lambda_k2 = 0.1 * jax.random.normal(ks[7], (DEPTH, DIFF_QK_DIM), f32)
    diff_subln_w = 1.0 + 0.02 * jax.random.normal(ks[8], (DEPTH, DIFF_V_DIM), f32)
    conv_w = jax.random.normal(ks[9], (DEPTH, SSD_CONV, SSD_CONV_DIM), f32) * SSD_CONV ** -0.5
    conv_b = 0.01 * jax.random.normal(ks[10], (DEPTH, SSD_CONV_DIM), f32)
    u = jax.random.uniform(ks[11], (DEPTH, SSD_HEADS), f32)
    dt0 = jnp.exp(u * (math.log(0.1) - math.log(0.001)) + math.log(0.001))
    dt_bias = dt0 + jnp.log(-jnp.expm1(-dt0))
    a_log = jnp.log(jax.random.uniform(ks[12], (DEPTH, SSD_HEADS), f32, 1.0, 16.0))
    d_skip = 1.0 + 0.1 * jax.random.normal(ks[13], (DEPTH, SSD_HEADS), f32)
    ssd_norm_w = 1.0 + 0.02 * jax.random.normal(ks[14], (DEPTH, SSD_WIDTH), f32)
    mem_norm_w = 1.0 + 0.02 * jax.random.normal(ks[15], (DEPTH, D_MODEL), f32)
    w_mem_kv = jax.random.normal(ks[16], (DEPTH, D_MODEL, 2 * XATTN_WIDTH), f32) * D_MODEL ** -0.5
    w_out = jax.random.normal(ks[17], (DEPTH, D_MIX, D_MODEL), f32) * D_MIX ** -0.5
    post_norm_w = 1.0 + 0.02 * jax.random.normal(ks[18], (DEPTH, D_MODEL), f32)
    return {'x': x, 'mem': mem, 'positions': positions, 'pre_norm_w': pre_norm_w,
            'w_in': w_in, 'lambda_q1': lambda_q1, 'lambda_k1': lambda_k1,
            'lambda_q2': lambda_q2, 'lambda_k2': lambda_k2, 'diff_subln_w': diff_subln_w,
            'conv_w': conv_w, 'conv_b': conv_b, 'dt_bias': dt_bias, 'a_log': a_log,
            'd_skip': d_skip, 'ssd_norm_w': ssd_norm_w, 'mem_norm_w': mem_norm_w,
            'w_mem_kv': w_mem_kv, 'w_out': w_out, 'post_norm_w': post_norm_w}


def reference(x, mem, positions, pre_norm_w, w_in, lambda_q1, lambda_k1, lambda_q2,
              lambda_k2, diff_subln_w, conv_w, conv_b, dt_bias, a_log, d_skip,
              ssd_norm_w, mem_norm_w, w_mem_kv, w_out, post_norm_w):
    cos, sin = rope_cos_sin(positions, DIFF_QK_DIM)
    h = x
    for i in range(DEPTH):
        lambda_init = 0.8 - 0.6 * math.exp(-0.3 * i)
        h = hybrid_layer(h, mem, cos, sin, lambda_init, pre_norm_w[i], w_in[i],
                         lambda_q1[i], lambda_k1[i], lambda_q2[i], lambda_k2[i],
                         diff_subln_w[i], conv_w[i], conv_b[i], dt_bias[i], a_log[i],
                         d_skip[i], ssd_norm_w[i], mem_norm_w[i], w_mem_kv[i],
                         w_out[i], post_norm_w[i])
    return h
```

```python
import numpy as np
import concourse.bass as bass
import concourse.mybir as mybir
from concourse.bass_utils import run_bass_kernel_spmd

F32 = mybir.dt.float32
BF16 = mybir.dt.bfloat16
I32 = mybir.dt.int32
U8 = mybir.dt.uint8
ALU = mybir.AluOpType
AF = mybir.ActivationFunctionType
AX = mybir.AxisListType

ENGS = ("pe", "act", "dve", "pool", "sp")
EPS = 1e-6
TWO_PI = float(2 * np.pi)


class Buf:
    _n = 0

    def __init__(self, P, name, shape, dtype, space="sbuf", nslot=1, at=None, keys=None):
        Buf._n += 1
        self.name = f"{name}_{Buf._n}"
        self.shape = list(shape)
        self.dtype = dtype
        self.nslot = nslot
        if space == "psum":
            self.t = P.nc.alloc_psum_tensor(self.name, self.shape, dtype)
            P.psum_names.add(self.name)
        elif at is not None:
            self.t = P.nc.alloc_sbuf_tensor_at(self.name, self.shape, dtype, offset=at)
        else:
            self.t = P.nc.alloc_sbuf_tensor(self.name, self.shape, dtype)
        self.rowlen = int(np.prod(self.shape[1:]))
        self.keys = keys if keys is not None else [(self.name, i) for i in range(nslot)]

    def k(self, *slots):
        if not slots:
            return list(self.keys)
        return [self.keys[s] for s in slots]

    def v(self, off=0, dims=None, p0=0, np_=None):
        if np_ is None:
            np_ = self.shape[0] - p0
        if dims is None:
            dims = [(1, self.rowlen - off)]
        ap = [[self.rowlen, np_]] + [[s, n] for (s, n) in dims]
        return bass.AP(self.t, p0 * self.rowlen + off, ap)

    def vb(self, off=0, dims=None, np_=None):
        rl = self.rowlen * 2
        if np_ is None:
            np_ = self.shape[0]
        if dims is None:
            dims = [(1, rl - off)]
        a = bass.AP(self.t, 0, [[self.rowlen, self.shape[0]], [1, self.rowlen]]).bitcast(BF16)
        t = a.tensor
        ap = [[rl, np_]] + [[s, n] for (s, n) in dims]
        return bass.AP(t, off, ap)


class Op:
    __slots__ = ("eng", "fn", "waits", "idx", "signal", "semval", "dma_sem", "dma_val", "is_dma", "tag")


class Prog:
    def __init__(self, nc):
        self.nc = nc
        self.ops = {e: [] for e in ENGS}
        self.last_w = {}
        self.readers = {}
        self.dma_sems = {}
        self.seen = {e: {} for e in ENGS}
        self.psum_names = set()
        self.names = {}

    def _keys(self, lst):
        out = []
        for x in lst:
            if isinstance(x, Buf):
                out.extend(x.keys)
            elif isinstance(x, list):
                out.extend(x)
            else:
                out.append(x)
        return out

    def op(self, eng, fn, reads=(), writes=(), dma=False, semkey=None):
        o = Op()
        import sys as _sys
        f_ = _sys._getframe(2)
        o.tag = "%d/%d" % (f_.f_lineno, f_.f_back.f_lineno if f_.f_back else 0)
        o.eng, o.fn, o.is_dma, o.signal, o.semval, o.dma_sem = eng, fn, dma, False, None, None
        o.idx = len(self.ops[eng])
        rk = self._keys(reads)
        wk = self._keys(writes)
        for k in rk:
            if isinstance(k, tuple) and k[0] in self.psum_names and k not in wk:
                wk.append(k)
        deps = []
        for k in rk:
            w = self.last_w.get(k)
            if w is not None:
                deps.append((w, "raw"))
        for k in wk:
            w = self.last_w.get(k)
            if w is not None:
                deps.append((w, "waw"))
            for r in self.readers.get(k, ()):
                deps.append((r, "war"))
        waits = {}
        for (d, kind) in deps:
            if d is o:
                continue
            if d.is_dma:
                waits[("dma", id(d))] = d
                continue
            if d.eng == eng and not dma:
                if eng == "pe":
                    continue
            prev = waits.get(("eng", d.eng))
            if prev is None or prev.idx < d.idx:
                waits[("eng", d.eng)] = d
        final = []
        for key, d in waits.items():
            if key[0] == "eng":
                s = self.seen[eng].get(d.eng, -1)
                if d.idx <= s:
                    continue
                self.seen[eng][d.eng] = d.idx
                d.signal = True
            else:
                if ("dmaseen", id(d)) in self.seen[eng]:
                    continue
                self.seen[eng][("dmaseen", id(d))] = True
            final.append(d)
        o.waits = final
        if dma:
            bkey = semkey
            if bkey is None:
                for x in list(writes) + list(reads):
                    if isinstance(x, Buf):
                        bkey = x.name
                        break
            assert bkey is not None
            ent = self.dma_sems.get(bkey)
            if ent is None:
                ent = [self.nc.alloc_semaphore("d_" + str(bkey)), 0]
                self.dma_sems[bkey] = ent
            ent[1] += 16
            o.dma_sem, o.dma_val = ent[0], ent[1]
        for k in rk:
            self.readers.setdefault(k, []).append(o)
        for k in wk:
            self.last_w[k] = o
            self.readers[k] = []
        self.ops[eng].append(o)
        return o

    def emit(self, final_waits=()):
        nc = self.nc
        self.esem = {e: nc.alloc_semaphore("e_" + e) for e in ENGS}
        for d in final_waits:
            if not d.is_dma:
                d.signal = True
        for e in ENGS:
            c = 0
            for o in self.ops[e]:
                if o.signal and not o.is_dma:
                    c += 1
                    o.semval = c
        prog = self

        def run(e, h):
            for o in prog.ops[e]:
                for d in o.waits:
                    if d.is_dma:
                        h.wait_ge(d.dma_sem, d.dma_val)
                    else:
                        h.wait_ge(prog.esem[d.eng], d.semval)
                ins = o.fn(h)
                try:
                    prog.names[ins.ins.name] = o.tag
                except Exception:
                    try:
                        prog.names[ins.name] = o.tag
                    except Exception:
                        pass
                if o.is_dma:
                    ins.then_inc(o.dma_sem, 16)
                elif o.signal:
                    ins.then_inc(prog.esem[e], 1)
            if e == "sp":
                for d in final_waits:
                    if d.is_dma:
                        h.wait_ge(d.dma_sem, d.dma_val)
                    else:
                        h.wait_ge(prog.esem[d.eng], d.semval)

        with nc.Block() as block:
            @block.tensor
            def _(h):
                run("pe", h)

            @block.scalar
            def _(h):
                run("act", h)

            @block.vector
            def _(h):
                run("dve", h)

            @block.gpsimd
            def _(h):
                run("pool", h)

            @block.sync
            def _(h):
                run("sp", h)
        return {e: len(self.ops[e]) for e in ENGS}


NHG = 38
PO_SSDW = 0
PO_SUBW = 512
PO_DSK = 640
PO_DTB = 648
PO_ALOG = 656
PO_LAM = 664
PO_CW = 920
PO_CB = 952
PO_INVF = 960
PO_INVL = 992
NPRM = 1024
C_ID, C_TRI, C_NEG, C_ONE = 0, 128, 256, 384

WSEQ = [26, 27, 28, 29, 16, 17, 18, 19, 20, 21] + list(range(16)) + [22, 24, 23, 25] + list(range(30, 38))


class _Stop(Exception):
    pass


def build(dbg=False, stage=99):
    nc = bass.Bass("TRN2", target_bir_lowering=False)
    P = Prog(nc)
    stores = []

    def stage_end(n):
        if stage == n:
            raise _Stop()

    def dram(name, shape, dt, kind="ExternalInput"):
        return nc.dram_tensor(name, shape, dt, kind=kind).ap()

    x_d = dram("x_tok", [2048, 2048], F32)
    mem_d = dram("memx", [256, 2048], F32)
    pos_d = dram("pos", [128, 16], I32)
    flag_d = dram("flag", [128, 1], F32)
    wg_d = dram("wg", [NHG, 128, 4096], F32)
    wdt_d = dram("wdt", [128, 128], F32)
    prm_d = dram("prm", [128, NPRM], F32)
    rows_d = dram("rows", [3, 2048], F32)
    cst_d = dram("cst", [128, 512], F32)
    out_d = dram("out", [1024, 2048], F32, kind="ExternalOutput")
    if dbg:
        dbg_d = dram("dbg", [128, 16 * 1024], BF16, kind="ExternalOutput")
        dbg2_d = dram("dbg2", [128, 16 * 2048], BF16, kind="ExternalOutput")

    def mm(out, lhsT, rhs, start, stop, R, W, skip=False):
        return P.op("pe", lambda h: h.matmul(out, lhsT=lhsT, rhs=rhs, start=start, stop=stop, skip_group_check=skip), R, W)

    def tr(out, in_, ident, R, W):
        return P.op("pe", lambda h: h.transpose(out=out, in_=in_, identity=ident), R, W)

    def act(out, in_, func, R, W, bias=None, scale=None, accum=None):
        kw = {}
        if bias is not None:
            kw["bias"] = bias
        if scale is not None:
            kw["scale"] = scale
        if accum is not None:
            kw["accum_out"] = accum
        return P.op("act", lambda h: h.activation(out=out, in_=in_, func=func, **kw), R, W)

    def tt(out, in0, in1, op, R, W, eng="dve"):
        return P.op(eng, lambda h: h.tensor_tensor(out=out, in0=in0, in1=in1, op=op), R, W)

    def ts(out, in0, s1, s2, op0, op1, R, W, eng="dve"):
        if op1 is None:
            return P.op(eng, lambda h: h.tensor_scalar(out=out, in0=in0, scalar1=s1, scalar2=None, op0=op0), R, W)
        return P.op(eng, lambda h: h.tensor_scalar(out=out, in0=in0, scalar1=s1, scalar2=s2, op0=op0, op1=op1), R, W)

    def stt(out, in0, scalar, in1, op0, op1, R, W):
        return P.op("dve", lambda h: h.scalar_tensor_tensor(out=out, in0=in0, scalar=scalar, in1=in1, op0=op0, op1=op1), R, W)

    def cp(out, in_, R, W, eng="dve"):
        if eng == "act":
            return P.op("act", lambda h: h.copy(out, in_), R, W)
        return P.op(eng, lambda h: h.tensor_copy(out, in_), R, W)

    def dma(out, in_, R, W, eng="sp", semkey=None):
        return P.op(eng, lambda h: h.dma_start(out=out, in_=in_), R, W, dma=True, semkey=semkey)

    def memset(ap, val, W, eng="dve"):
        return P.op(eng, lambda h: h.memset(ap, val), (), W)

    def recip(out, in_, R, W):
        return P.op("dve", lambda h: h.reciprocal(out, in_), R, W)

    hT = Buf(P, "hT", [128, 16 * 2048], BF16, nslot=16)
    mixT = Buf(P, "mixT", [128, 16 * 1024], BF16, nslot=128)
    wbig = nc.alloc_sbuf_tensor("wbig", [128, 3 * 4096], BF16)

    class SubBuf(Buf):
        def __init__(self, name, parent_t, parent_rowlen, col0, ncol):
            Buf._n += 1
            self.name = f"{name}_{Buf._n}"
            self.shape = [128, ncol]
            self.t = parent_t
            self.rowlen = ncol
            self.prow = parent_rowlen
            self.col0 = col0
            self.nslot = 1
            self.keys = [(self.name, 0)]

        def v(self, off=0, dims=None, p0=0, np_=None):
            if np_ is None:
                np_ = 128 - p0
            if dims is None:
                dims = [(1, self.rowlen - off)]
            ap = [[self.prow, np_]] + [[s_, n_] for (s_, n_) in dims]
            return bass.AP(self.t, p0 * self.prow + self.col0 + off, ap)

    wr = [SubBuf(f"wr{i}", wbig, 3 * 4096, i * 4096, 4096) for i in range(3)]
    ARENA = 59392
    arena = nc.alloc_sbuf_tensor("arena", [128, ARENA], U8)
    abase = nc.lookup_mloc(arena).addr
    hbase = nc.lookup_mloc(hT.t).addr

    def aview(name, shape, dt, off, nbytes):
        keys = [("arena", i) for i in range(off // 1024, (off + nbytes - 1) // 1024 + 1)]
        return Buf(P, name, shape, dt, at=abase + off, keys=keys)

    XS = [aview(f"xs{i}", [128, 2048], F32, i * 8192, 8192) for i in range(2)] + [aview("xs2", [128, 2048], F32, 45056, 8192)]
    XN = [aview(f"xn{i}", [128, 2048], BF16, 16384 + i * 4096, 4096) for i in range(2)]
    bigrow = aview("bigrow", [128, 2048], F32, 24576, 8192)
    memT = aview("memT", [128, 16 * 256], BF16, 32768, 8192)
    junk = aview("junk", [128, 2048], BF16, 40960, 4096)
    tmpA = [aview(f"tmpA{i}", [128, 512], F32, 45056 + i * 2048, 2048) for i in range(4)]
    xbcF = aview("xbcF", [128, 8 * 2048], BF16, 0, 32768)
    RP = 256
    raw = Buf(P, "raw", [128, RP + 2048], F32, at=abase + 32768)
    raw.keys = [("arena", 32 + i) for i in range(0, 9)]
    rawk = lambda G: [("arena", 33 + 2 * G), ("arena", 34 + 2 * G)]
    rawh = lambda G: [("arena", 32 + 2 * G)] + rawk(G)
    sA = 41984
    accs = [aview(f"acc{i}", [128, 512], F32, sA + i * 2048, 2048) for i in range(2)]
    ytm = [aview(f"ytm{i}", [128, 512], F32, sA + 4096 + i * 2048, 2048) for i in range(2)]
    KT = [aview(f"KT{i}", [128, 2048], BF16, i * 4096, 4096) for i in range(2)]
    VA = [aview(f"VA{i}", [128, 16 * 130], BF16, 16384 + i * 5120, 4160) for i in range(2)]
    QT = [aview(f"QT{i}", [128, 2048], BF16, 26624 + i * 4096, 4096) for i in range(2)]
    GT = [aview(f"GT{i}", [128, 8 * 128], BF16, 34816 + i * 2048, 2048) for i in range(2)]
    PT = [aview(f"PT{i}", [128, 512], BF16, 38912 + i * 1024, 1024) for i in range(4)]
    ropeA = aview("ropeA", [128, 256], F32, 43008, 1024)
    ropeB = aview("ropeB", [128, 256], F32, 44032, 1024)
    kr01s = [aview("kr01a", [128, 256], BF16, 45056, 512), aview("kr01b", [128, 256], BF16, 46080, 512)]
    fin = [aview(f"fin{i}", [128, 256], F32, 47104 + i * 1024, 1024) for i in range(3)]
    dout01 = aview("dout01", [128, 256], BF16, 50176, 512)
    qx = aview("qx", [128, 256], BF16, 52224, 512)
    Oev = [aview(f"oev{i}", [128, 258], F32, 53248 + i * 2048, 1032) for i in range(2)]
    sg = [aview(f"sg{i}", [128, 128], F32, 57344 + i * 1024, 512) for i in range(2)]
    xoff = [8192, 0]
    XQ = [[aview(f"xq{hp}{hd}", [128, 1024], BF16, xoff[hp] + hd * 2048, 2048) for hd in range(2)] for hp in range(2)]
    XG = [[aview(f"xg{hp}{hd}", [128, 1024], BF16, xoff[hp] + 4096 + hd * 2048, 2048) for hd in range(2)] for hp in range(2)]
    YB = [Buf(P, f"yb{j}", [128, 2048], F32, at=hbase + j * 8192) for j in range(8)]

    cst = Buf(P, "cst", [128, 512], F32)
    identb = Buf(P, "identb", [128, 128], BF16)
    trib = Buf(P, "trib", [128, 128], BF16)
    negmb = Buf(P, "negmb", [128, 128], BF16)
    prm = Buf(P, "prm", [128, NPRM], F32)
    cosb = Buf(P, "cosb", [128, 512], F32)
    sinb = Buf(P, "sinb", [128, 512], F32)
    flag = Buf(P, "flag", [128, 1], F32)
    posi = Buf(P, "posi", [128, 16], I32)
    posf = Buf(P, "posf", [128, 16], F32)
    ki = Buf(P, "ki", [128, 512], I32)
    sm = Buf(P, "sm", [128, 64], F32, nslot=64)
    ss = Buf(P, "ss", [128, 32], F32, nslot=32)
    lnv = Buf(P, "lnv", [128, 32], F32, nslot=32)
    rstd = Buf(P, "rstd", [128, 32], F32, nslot=32)
    mkT = Buf(P, "mkT", [128, 4 * 256], BF16)
    mvA = Buf(P, "mvA", [128, 2 * 4 * 130], BF16)
    wdtf = Buf(P, "wdtf", [128, 128], F32)
    wdtb = Buf(P, "wdtb", [128, 128], BF16)
    dtb = Buf(P, "dtb", [128, 128], F32)
    adt = Buf(P, "adt", [128, 128], F32)
    Ahat = Buf(P, "Ahat", [128, 8], F32)
    Sst = Buf(P, "Sst", [128, 512], F32)
    Sbf = Buf(P, "Sbf", [128, 512], BF16)
    acs = Buf(P, "acs", [128, 16], F32)
    sc8 = [Buf(P, f"sc8_{i}", [128, 8], F32) for i in range(6)]
    xsT = Buf(P, "xsT", [128, 512], BF16)
    BTb = Buf(P, "BTb", [128, 256], BF16)
    Xb = Buf(P, "Xb", [128, 512], BF16)
    Xd = Buf(P, "Xd", [128, 512], BF16)
    zs = Buf(P, "zs", [128, 512], BF16)
    sob = Buf(P, "sob", [128, 512], BF16)
    ps = [Buf(P, f"ps{i}", [128, 512], F32, space="psum") for i in range(8)]

    try:
        dma(cst.v(), cst_d, [], [cst])
        dma(prm.v(), prm_d, [], [prm])
        dma(flag.v(), flag_d, [], [flag])
        dma(posi.v(), pos_d, [], [posi])
        dma(wdtf.v(), wdt_d, [], [wdtf])
        dma(bigrow.v(), rows_d[0:1, :].partition_broadcast(128), [], [bigrow])
        cp(identb.v(), cst.v(C_ID, [(1, 128)]), [cst], [identb])
        cp(trib.v(), cst.v(C_TRI, [(1, 128)]), [cst], [trib])
        cp(negmb.v(), cst.v(C_NEG, [(1, 128)]), [cst], [negmb])
        cp(wdtb.v(), wdtf.v(), [wdtf], [wdtb])
        cp(posf.v(), posi.v(), [posi], [posf])
        u0, u1, u2, u3 = tmpA
        tt(u0.v(0, [(32, 16), (1, 32)]), posf.v(0, [(1, 16), (0, 32)]), prm.v(PO_INVF, [(0, 16), (1, 32)]), ALU.mult,
           [posf, prm], [u0])
        tt(u1.v(0, [(32, 16), (1, 32)]), posf.v(0, [(1, 16), (0, 32)]), prm.v(PO_INVL, [(0, 16), (1, 32)]), ALU.mult,
           [posf, prm], [u1])
        tt(u0.v(), u0.v(), u1.v(), ALU.add, [u0, u1], [u0])
        for (shift, tab) in ((0.0, sinb), (0.25, cosb)):
            ts(u1.v(), u0.v(), shift, None, ALU.add, None, [u0], [u1])
            cp(ki.v(), u1.v(), [u1], [ki])
            cp(u2.v(), ki.v(), [ki], [u2])
            tt(u3.v(), u1.v(), u2.v(), ALU.subtract, [u1, u2], [u3])
            stt(u2.v(), u3.v(), 0.5, u3.v(), ALU.is_gt, ALU.subtract, [u3], [u2])
            act(tab.v(), u2.v(), AF.Sin, [u2], [tab], scale=-TWO_PI)
        LAM_INIT = 0.2
        for i in range(2):
            tt(u1.v(0, [(1, 64)]), prm.v(PO_LAM + i * 128, [(1, 64)]), prm.v(PO_LAM + i * 128 + 64, [(1, 64)]), ALU.mult,
               [prm], [u1])
            P.op("dve", lambda h, o=sm.v(i, [(1, 1)]), a=u1.v(0, [(1, 64)]): h.tensor_reduce(out=o, in_=a, axis=AX.X, op=ALU.add),
                 [u1], sm.k(i))
            act(sm.v(2 + i, [(1, 1)]), sm.v(i, [(1, 1)]), AF.Exp, sm.k(i), sm.k(2 + i))
        tt(sm.v(4, [(1, 1)]), sm.v(3, [(1, 1)]), sm.v(2, [(1, 1)]), ALU.subtract, sm.k(2, 3), sm.k(4))
        ts(sm.v(4, [(1, 1)]), sm.v(4, [(1, 1)]), -LAM_INIT, None, ALU.add, None, sm.k(4), sm.k(4))
        NLAM = 4
        act(Ahat.v(), prm.v(PO_ALOG, [(1, 8)]), AF.Exp, [prm], [Ahat])
        ts(Ahat.v(), Ahat.v(), -1.0, None, ALU.mult, None, [Ahat], [Ahat])

        wstate = {"n": 0}

        def wissue(i, after=()):
            if i < len(WSEQ):
                b = wr[i % 3]
                dma(b.v(), wg_d[WSEQ[i]], list(after), [b], eng="pool")

        def wnext(expect):
            i = wstate["n"]
            assert WSEQ[i] == expect, (i, WSEQ[i], expect)
            wstate["n"] = i + 1
            return wr[i % 3], i

        def wdone(i):
            wissue(i + 3)

        wstart = {"done": False}

        def norm_tiles(src_d, ntile, dstbuf, dst_tok_stride, slot_of, col0):
            def stats(t):
                xs, xn = XS[t % 3], XN[t % 2]
                c = col0 + t
                dma(xs.v(), src_d[t * 128:(t + 1) * 128, :], [], [xs])
                act(junk.v(), xs.v(), AF.Square, [xs], [junk] + ss.k(c), accum=ss.v(c, [(1, 1)]))
                act(lnv.v(c, [(1, 1)]), ss.v(c, [(1, 1)]), AF.Ln, ss.k(c), lnv.k(c), scale=1.0 / 2048, bias=EPS)
                act(rstd.v(c, [(1, 1)]), lnv.v(c, [(1, 1)]), AF.Exp, lnv.k(c), rstd.k(c), scale=-0.5)

            def rest(t):
                xs, xn = XS[t % 3], XN[t % 2]
                c = col0 + t
                stt(xn.v(), xs.v(), rstd.v(c, [(1, 1)]), bigrow.v(), ALU.mult, ALU.mult, [xs, bigrow] + rstd.k(c), [xn])
                for q in range(4):
                    pb = ps[(t * 4 + q) % 4]
                    for i in range(4):
                        kc = q * 4 + i
                        tr(pb.vb(i * 128, [(1, 128)]), xn.v(kc * 128, [(1, 128)]), identb.v(), [xn, identb], [pb])
                    o = dstbuf.v(q * 4 * dst_tok_stride + t * 128, [(dst_tok_stride, 4), (1, 128)])
                    cp(o, pb.vb(0, [(128, 4), (1, 128)]), [pb], slot_of(t), eng=("act" if q % 2 else "dve"))

            stats(0)
            if ntile > 1:
                stats(1)
            for t in range(ntile):
                if t + 2 < ntile:
                    stats(t + 2)
                rest(t)
                if t == 2 and not wstart["done"]:
                    wstart["done"] = True
                    for i_ in range(3):
                        wissue(i_, after=[XN[t % 2]])

        norm_tiles(x_d, 16, hT, 2048, lambda t: hT.k(t), 0)
        dma(bigrow.v(), rows_d[1:2, :].partition_broadcast(128), [], [bigrow])
        norm_tiles(mem_d, 2, memT, 256, lambda t: [memT], 16)

        stage_end(1)
        memset(mvA.v(), 1.0, [mvA])
        stage_end(1.2)
        pcount = {"n": 0}

        def nextps(lo=0, hi=2):
            pcount["n"] += 1
            return ps[lo + pcount["n"] % (hi - lo)]

        for hp in range(2):
            W, wi = wnext(26 + hp)
            for hd in range(2):
                hx = 2 * hp + hd
                pb = nextps()
                for kc in range(16):
                    mm(pb.v(0, [(1, 256)]), W.v(kc * 256 + hd * 128, [(1, 128)]), memT.v(kc * 256, [(1, 256)]),
                       kc == 0, kc == 15, [W, memT], [pb])
                cp(mkT.v(hx * 256, [(1, 256)]), pb.v(0, [(1, 256)]), [pb], [mkT], eng="act")
            wdone(wi)
        stage_end(1.5)
        for hp in range(2):
            W, wi = wnext(28 + hp)
            for mt in range(2):
                pb = nextps()
                for kc in range(16):
                    mm(pb.v(0, [(1, 256)]), memT.v(kc * 256 + mt * 128, [(1, 128)]), W.v(kc * 256, [(1, 256)]),
                       kc == 0, kc == 15, [W, memT], [pb])
                o = mvA.v((mt * 4 + 2 * hp) * 130, [(130, 2), (1, 128)])
                cp(o, pb.v(0, [(128, 2), (1, 128)]), [pb], [mvA], eng="act")
            wdone(wi)

        stage_end(2)
        memset(raw.v(RP - 4, [(1, 4)]), 0.0, [("arena", 32)])
        memset(Sst.v(), 0.0, [Sst])
        memset(Sbf.v(), 0.0, [Sbf])
        for t in range(16):
            for kc in range(16):
                mm(ps[7].v(t * 8, [(1, 8)]), hT.v(kc * 2048 + t * 128, [(1, 128)]), wdtb.v(kc * 8, [(1, 8)]),
                   kc == 0, kc == 15, [wdtb] + hT.k(t), [ps[7]])
        tt(dtb.v(0, [(8, 16), (1, 8)]), ps[7].v(0, [(8, 16), (1, 8)]), prm.v(PO_DTB, [(0, 16), (1, 8)]), ALU.add,
           [ps[7], prm], [dtb])
        act(dtb.v(), dtb.v(), AF.Exp, [dtb], [dtb])
        act(dtb.v(), dtb.v(), AF.Ln, [dtb], [dtb], bias=1.0)
        tt(adt.v(0, [(8, 16), (1, 8)]), dtb.v(0, [(8, 16), (1, 8)]), Ahat.v(0, [(0, 16), (1, 8)]), ALU.mult, [dtb, Ahat], [adt])
        stage_end(2.2)
        for hg in range(16, 20):
            W, wi = wnext(hg)
            for cc in range(2):
                c = 2 * (hg - 16) + cc
                for G in range(4):
                    if c >= 6 and G == 0:
                        continue
                    pb = nextps()
                    if c >= 6 and G == 1:
                        for kc in range(16):
                            mm(pb.v(0, [(1, 128)]), W.v(kc * 256 + cc * 128, [(1, 128)]), hT.v(kc * 2048 + 896, [(1, 128)]),
                               kc == 0, kc == 15, [W] + hT.k(7), [pb])
                        cp(raw.v(RP + 896, [(1, 128)]), pb.v(0, [(1, 128)]), [pb], rawk(1), eng="act")
                        continue
                    for kc in range(16):
                        mm(pb.v(), W.v(kc * 256 + cc * 128, [(1, 128)]), hT.v(kc * 2048 + G * 512, [(1, 512)]),
                           kc == 0, kc == 15, [W] + hT.k(4 * G, 4 * G + 1, 4 * G + 2, 4 * G + 3), [pb])
                    cp(raw.v(RP + G * 512, [(1, 512)]), pb.v(), [pb], rawk(G), eng="act")
                    a = accs[G % 2]
                    base = RP + G * 512 - 3
                    ts(a.v(), raw.v(base, [(1, 512)]), prm.v(PO_CW + c * 4, [(1, 1)]), prm.v(PO_CB + c, [(1, 1)]),
                       ALU.mult, ALU.add, rawh(G) + [prm], [a])
                    for k in range(1, 4):
                        stt(a.v(), raw.v(base + k, [(1, 512)]), prm.v(PO_CW + c * 4 + k, [(1, 1)]), a.v(),
                            ALU.mult, ALU.add, rawh(G) + [prm, a], [a])
                    act(xbcF.v(c * 2048 + G * 512, [(1, 512)]), a.v(), AF.Silu, [a], [xbcF])
                if cc == 1:
                    wdone(wi)
        stage_end(2.5)
        Wz2 = [wnext(20), wnext(21)]
        Wz = [w_[0] for w_ in Wz2]
        Sprev = [aview(f"sprev{j}", [128, 512], BF16, 32768 + j * 1024, 1024) for j in range(8)]
        acsA = aview("acsA", [128, 256], F32, 40960, 1024)
        Xd2 = aview("Xd2", [128, 512], BF16, 54272, 1024)
        BTb2 = aview("BTb2", [128, 256], BF16, 55296, 512)
        xsT2 = aview("xsT2", [128, 512], BF16, 56320, 1024)
        sc8b = [aview(f"sc8b{i}", [128, 8], F32, 57344 + i * 1024, 32) for i in range(2)]
        XDs, BTs, xsTs = [Xd, Xd2], [BTb, BTb2], [xsT, xsT2]
        abc4 = aview("abc4", [128, 512], F32, 44032, 2048)
        tmpmAB = [aview("tmpmA", [128, 512], F32, 50176, 2048), aview("tmpmB", [128, 512], F32, 52224, 2048)]
        Gs = [Xd, Xd2]
        TMP8s, DTEs, XDTs, CDs = [sc8[0], sc8[4]], [sc8[1], sc8[5]], [sc8[2], sc8b[0]], [sc8[3], sc8b[1]]
        NACs = [Buf(P, "nac0", [128, 8], F32), Buf(P, "nac1", [128, 8], F32)]
        EACSs = [Buf(P, "eacs0", [128, 8], F32), Buf(P, "eacs1", [128, 8], F32)]

        for t in range(16):
            p = t % 2
            psA, psT, psSt = (ps[0], ps[1], ps[2]) if p == 0 else (ps[3], ps[4], ps[5])
            TMP8, DTE, XDT, CD, Xd_, BT_ = TMP8s[p], DTEs[p], XDTs[p], CDs[p], XDs[p], BTs[p]
            ac = acsA.v(t * 16, [(1, 8)])
            tot = acsA.v(t * 16 + 8, [(1, 8)])
            akey = [acsA]
            mm(psA.v(0, [(1, 8)]), cst.v(C_TRI, [(1, 128)]), adt.v(t * 8, [(1, 8)]), True, True, [cst, adt], [psA])
            mm(psA.v(8, [(1, 8)]), cst.v(C_ONE, [(1, 128)]), adt.v(t * 8, [(1, 8)]), True, True, [cst, adt], [psA])
            cp(acsA.v(t * 16, [(1, 16)]), psA.v(0, [(1, 16)]), [psA], akey)
            tt(TMP8.v(), tot, ac, ALU.subtract, akey, [TMP8])
            act(DTE.v(), TMP8.v(), AF.Exp, [TMP8], [DTE])
            tt(XDT.v(), dtb.v(t * 8, [(1, 8)]), DTE.v(), ALU.mult, [dtb, DTE], [XDT])
            act(CD.v(), tot, AF.Exp, akey, [CD])
            for i in range(6):
                tr(psT.vb(i * 128, [(1, 128)]), xbcF.v(i * 2048 + t * 128, [(1, 128)]), identb.v(), [xbcF, identb], [psT])
            tt(Xd_.v(0, [(64, 8), (1, 64)]), psT.vb(0, [(64, 8), (1, 64)]), XDT.v(0, [(1, 8), (0, 64)]), ALU.mult,
               [psT, XDT], [Xd_])
            cp(BT_.v(), psT.vb(512, [(1, 256)]), [psT], [BT_], eng="act")
            if t < 15:
                for g in range(2):
                    mm(psSt.v(g * 256, [(1, 256)]), BT_.v(g * 128, [(1, 128)]), Xd_.v(g * 256, [(1, 256)]), True, True,
                       [BT_, Xd_], [psSt])
                tt(Sst.v(0, [(64, 8), (1, 64)]), Sst.v(0, [(64, 8), (1, 64)]), CD.v(0, [(1, 8), (0, 64)]), ALU.mult,
                   [Sst, CD], [Sst])
                tt(Sst.v(), Sst.v(), psSt.v(), ALU.add, [Sst, psSt], [Sst])
                if t == 7:
                    ts(Sst.v(), Sst.v(), flag.v(), None, ALU.mult, None, [Sst, flag], [Sst])
                if t >= 7:
                    cp(Sprev[t - 7].v(), Sst.v(), [Sst], [Sprev[t - 7]], eng="act")

        zss = [zs, Buf(P, "zs2", [128, 512], BF16)]

        def y_front(j):
            t = 8 + j
            p = j % 2
            xsT_, NAC, EACS = xsTs[p], NACs[p], EACSs[p]
            ac = acsA.v(t * 16, [(1, 8)])
            for zi in range(2):
                for kc in range(16):
                    mm(ps[7].v(zi * 256, [(1, 256)]), hT.v(kc * 2048 + t * 128, [(1, 128)]), Wz[zi].v(kc * 256, [(1, 256)]),
                       kc == 0, kc == 15, [Wz[zi]] + hT.k(t), [ps[7]])
            for i in range(4):
                tr(ps[1].vb(i * 128, [(1, 128)]), xbcF.v(i * 2048 + t * 128, [(1, 128)]), identb.v(), [xbcF, identb], [ps[1]])
            cp(xsT_.v(), ps[1].vb(0, [(1, 512)]), [ps[1]], [xsT_], eng="act")
            tt(Xb.v(0, [(64, 8), (1, 64)]), ps[1].vb(0, [(64, 8), (1, 64)]), dtb.v(t * 8, [(1, 8), (0, 64)]), ALU.mult,
               [ps[1], dtb], [Xb])
            ts(NAC.v(), ac, -1.0, None, ALU.mult, None, [acsA], [NAC])
            act(EACS.v(), ac, AF.Exp, [acsA], [EACS])
            for g in range(2):
                mm(ps[0].v(128 + g * 128, [(1, 128)]), xbcF.v((4 + g) * 2048 + t * 128, [(1, 128)]),
                   xbcF.v((6 + g) * 2048 + t * 128, [(1, 128)]), True, True, [xbcF], [ps[0]])
            v4 = [(128, 4), (1, 128)]
            for b in range(2):
                for hh in range(4):
                    mm(ps[3 + b].v(hh * 128, [(1, 128)]), adt.v(t * 8 + 4 * b + hh, [(0, 128)]), cst.v(C_TRI, [(1, 128)]),
                       True, True, [adt, cst], [ps[3 + b]])
            for b in range(2):
                tt(tmpmAB[b].v(0, v4), ps[3 + b].v(0, v4), cst.v(C_NEG, [(0, 4), (1, 128)]), ALU.add, [ps[3 + b], cst],
                   [tmpmAB[b]])
            for b in range(2):
                tm_ = tmpmAB[b]
                for hh in range(4):
                    act(tm_.v(hh * 128, [(1, 128)]), tm_.v(hh * 128, [(1, 128)]), AF.Exp, [tm_, NAC], [tm_],
                        bias=NAC.v(4 * b + hh, [(1, 1)]))
            for b in range(2):
                tt(Gs[b].v(0, v4), ps[0].v(128 + b * 128, [(0, 4), (1, 128)]), tmpmAB[b].v(0, v4), ALU.mult,
                   [ps[0], tmpmAB[b]], [Gs[b]])
            for b in range(2):
                for hh in range(4):
                    h = 4 * b + hh
                    mm(ps[5].v(h * 64, [(1, 64)]), Gs[b].v(hh * 128, [(1, 128)]), Xb.v(h * 64, [(1, 64)]), True, True,
                       [Gs[b], Xb], [ps[5]])
            for g in range(2):
                mm(ps[6].v(g * 256, [(1, 256)]), xbcF.v((6 + g) * 2048 + t * 128, [(1, 128)]),
                   Sprev[j].v(g * 256, [(1, 256)]), True, True, [xbcF, Sprev[j]], [ps[6]])

        y1s = [ytm[0], accs[0]]

        def y_tail_a(j):
            p = j % 2
            xsT_, EACS = xsTs[p], EACSs[p]
            y1, y2 = y1s[p], ytm[1]
            tt(y1.v(0, [(64, 8), (1, 64)]), ps[6].v(0, [(64, 8), (1, 64)]), EACS.v(0, [(1, 8), (0, 64)]), ALU.mult,
               [ps[6], EACS], [y1])
            tt(y2.v(0, [(64, 8), (1, 64)]), xsT_.v(0, [(64, 8), (1, 64)]), prm.v(PO_DSK, [(1, 8), (0, 64)]), ALU.mult,
               [xsT_, prm], [y2])
            tt(y1.v(), y1.v(), y2.v(), ALU.add, [y1, y2], [y1])
            tt(y1.v(), ps[5].v(), y1.v(), ALU.add, [ps[5], y1], [y1])
            act(y2.v(), ps[7].v(), AF.Exp, [ps[7]], [y2], scale=-1.0)
            act(y2.v(), y2.v(), AF.Ln, [y2], [y2], bias=1.0)
            act(y2.v(), y2.v(), AF.Exp, [y2], [y2], scale=-1.0)
            tt(zss[p].v(), ps[7].v(), y2.v(), ALU.mult, [ps[7], y2], [zss[p]])

        def y_tail_b(j):
            t = 8 + j
            p = j % 2
            y1, y2 = y1s[p], ytm[1]
            tt(y1.v(), y1.v(), zss[p].v(), ALU.mult, [y1, zss[p]], [y1])
            for g in range(2):
                c = 18 + g
                act(y2.v(g * 256, [(1, 256)]), y1.v(g * 256, [(1, 256)]), AF.Square, [y1], [y2] + ss.k(c),
                    accum=ss.v(c, [(1, 1)]))
                act(lnv.v(c, [(1, 1)]), ss.v(c, [(1, 1)]), AF.Ln, ss.k(c), lnv.k(c), scale=1.0 / 256, bias=EPS)
                act(rstd.v(c, [(1, 1)]), lnv.v(c, [(1, 1)]), AF.Exp, lnv.k(c), rstd.k(c), scale=-0.5)
                stt(sob.v(g * 256, [(1, 256)]), y1.v(g * 256, [(1, 256)]), rstd.v(c, [(1, 1)]),
                    prm.v(PO_SSDW + g * 256, [(1, 256)]), ALU.mult, ALU.mult, [y1, prm] + rstd.k(c), [sob])
            for i in range(4):
                tr(ps[2].vb(i * 128, [(1, 128)]), sob.v(i * 128, [(1, 128)]), identb.v(), [sob, identb], [ps[2]])
            cp(mixT.v(8 * 1024 + j * 128, [(1024, 4), (1, 128)]), ps[2].vb(0, [(128, 4), (1, 128)]), [ps[2]],
               [mixT.k((8 + i) * 8 + j)[0] for i in range(4)], eng="act")

        y_front(0)
        y_tail_a(0)
        for j in range(8):
            if j + 1 < 8:
                y_front(j + 1)
            y_tail_b(j)
            if j + 1 < 8:
                y_tail_a(j + 1)

        stage_end(3)
        wdone(Wz2[0][1])
        wdone(Wz2[1][1])
        for b in range(2):
            memset(QT[b].v(), 0.0, [QT[b]])
            memset(VA[b].v(), 1.0, [VA[b]])
            cp(VA[b].v(128, [(130, 8)]), flag.v(0, [(0, 8)]), [flag], [VA[b]])

        pcnt = {"p": 0}
        RC0, RC1, NL, SSQ, LNQ, RSQ = 8, 10, 12, 14, 16, 18

        carry = []
        gn = {"n": 0}

        def attention_gen(nmap, n_kb_base, causal, scale, Kfn, Vfn, fin_head, fin_tail):
            its = []
            for G in range(4):
                nkb = (n_kb_base + 2 * G + 2) if causal else n_kb_base
                for kb in range(nkb):
                    its.append((G, kb, nkb))
            pts = {}
            deferred = carry

            def geom(G, kb):
                i = kb - (n_kb_base + 2 * G) if causal else -1
                q0 = 128 if i >= 1 else 0
                return i, q0, 256 - q0

            def qk(n):
                G, kb, nkb = its[n]
                i, q0, nq = geom(G, kb)
                pcnt["p"] += 1
                pt = PT[pcnt["p"] % 4]
                pts[n] = pt
                pS = ps[4 + n % 2]

                def addmask(c):
                    mm(pS.v(c * 256 + i * 128, [(1, 128)]), identb.v(), negmb.v(), False, True, [identb, negmb], [pS],
                       skip=True)

                if nmap == 2 and q0 > 0:
                    for c in range(2):
                        lhsT, rhs, RR = Kfn(kb, G, q0, nq, c)
                        mm(pS.v(c * 256 + q0, [(1, nq)]), lhsT, rhs, True, False, RR, [pS], skip=True)
                        addmask(c)
                else:
                    lhsT, rhs, RR = Kfn(kb, G, q0, nq)
                    mm(pS.v(q0, [(1, nmap * 256 - q0)]), lhsT, rhs, True, i < 0, RR, [pS], skip=(i >= 0))
                    if i >= 0:
                        for c in range(nmap):
                            addmask(c)
                act(pt.v(q0, [(256, nmap), (1, nq)]), pS.v(q0, [(256, nmap), (1, nq)]), AF.Exp, [pS], [pt], scale=scale)

            qk(0)
            for n in range(len(its)):
                G, kb, nkb = its[n]
                i, q0, nq = geom(G, kb)
                if n + 1 < len(its):
                    qk(n + 1)
                yield
                gn["n"] += 1
                for ent in list(deferred):
                    if ent[0] <= gn["n"]:
                        ent[1]()
                        deferred.remove(ent)
                pt = pts.pop(n)
                for c in range(nmap):
                    for jj in range(2):
                        if jj * 128 < q0:
                            continue
                        last = (n_kb_base + 2 * G + jj) if causal else (nkb - 1)
                        va, RV = Vfn(kb)
                        mm(ps[6 + c].v(jj * 129, [(1, 129)]), pt.v(c * 256 + jj * 128, [(1, 128)]), va,
                           kb == 0 and jj == 0, kb == last, [pt] + RV, [ps[6 + c]], skip=True)
                if kb == nkb - 1:
                    for ent in deferred:
                        ent[1]()
                    del deferred[:]
                    part2 = fin_head(G)
                    dl = 0
                    if part2 is not None:
                        deferred.append((gn["n"] + 3, part2))
                        dl = 3
                    for jj in range(2):
                        deferred.append((gn["n"] + 6 + dl, lambda G=G, jj=jj: fin_tail(G, jj)))

        sgc = {"n": 0}

        def silu_exp(dst_ap, dstbuf, src_ap, srcbuf):
            sgc["n"] += 1
            g = sg[sgc["n"] % 2]
            act(g.v(), src_ap, AF.Exp, [srcbuf], [g], scale=-1.0)
            ts(g.v(), g.v(), 1.0, None, ALU.add, None, [g], [g])
            recip(g.v(), g.v(), [g], [g])
            tt(dst_ap, src_ap, g.v(), ALU.mult, [srcbuf, g], [dstbuf])

        def run_gen(g):
            for _ in g:
                pass

        def diff_proj_gen(h):
            hb = h % 2
            W0, wi0 = wnext(2 * h)
            W1, wi1 = wnext(2 * h + 1)
            sl0, sl1 = wi0 % 3, wi1 % 3
            kt, vaB, qt, gt = KT[hb], VA[hb], QT[hb], GT[hb]
            pend1, pend2 = [], []

            def wrhs(kc, ncol):
                return bass.AP(wbig, sl0 * 4096 + kc * 256, [[3 * 4096, 128], [(sl1 - sl0) * 4096, 2], [1, ncol]])

            def evac1(t, pA):
                own = t >= 8
                j = t - 8
                kr01 = kr01s[t % 2]
                ng = 8 if own else 4
                nk = 256 if own else 128
                if own:
                    cp(vaB.v(t * 130, [(1, 128)]), pA.v(nk, [(1, 128)]), [pA], [vaB], eng="act")
                else:
                    ts(vaB.v(t * 130, [(1, 128)]), pA.v(nk, [(1, 128)]), flag.v(), None, ALU.mult, None, [pA, flag], [vaB])
                if own:
                    sgc["n"] += 1
                    gbuf = sg[sgc["n"] % 2]
                    act(gbuf.v(), pA.v(384, [(1, 128)]), AF.Exp, [pA], [gbuf], scale=-1.0)
                src = pA.v(0, [(32, ng), (1, 32)])
                tt(ropeA.v(0, [(32, ng), (1, 32)]), src, cosb.v(t * 32, [(0, ng), (1, 32)]), ALU.mult, [pA, cosb], [ropeA])
                tt(ropeB.v(0, [(32, ng), (1, 32)]), src, sinb.v(t * 32, [(0, ng), (1, 32)]), ALU.mult, [pA, sinb], [ropeB])
                tt(kr01.v(0, [(64, ng // 2), (1, 32)]), ropeA.v(0, [(64, ng // 2), (1, 32)]),
                   ropeB.v(32, [(64, ng // 2), (1, 32)]), ALU.subtract, [ropeA, ropeB], [kr01])
                tt(kr01.v(32, [(64, ng // 2), (1, 32)]), ropeA.v(32, [(64, ng // 2), (1, 32)]),
                   ropeB.v(0, [(64, ng // 2), (1, 32)]), ALU.add, [ropeA, ropeB], [kr01])
                if own:
                    ts(gbuf.v(), gbuf.v(), 1.0, None, ALU.add, None, [gbuf], [gbuf])
                    recip(gbuf.v(), gbuf.v(), [gbuf], [gbuf])
                    tt(gt.v(j * 128, [(1, 128)]), pA.v(384, [(1, 128)]), gbuf.v(), ALU.mult, [pA, gbuf], [gt])

            def evac2(t):
                own = t >= 8
                j = t - 8
                kr01 = kr01s[t % 2]
                tr(ps[3].vb(0, [(1, 128)]), kr01.v(0, [(1, 128)]), identb.v(), [kr01, identb], [ps[3]])
                if own:
                    tr(ps[3].vb(128, [(1, 128)]), kr01.v(128, [(1, 128)]), identb.v(), [kr01, identb], [ps[3]])
                cp(kt.v(t * 128, [(1, 128)]), ps[3].vb(0, [(1, 128)]), [ps[3]], [kt], eng="act")
                if own:
                    G_, jj_ = j // 2, j % 2
                    for c in range(2):
                        cp(qt.v(G_ * 512 + c * 256 + jj_ * 128, [(1, 128)], p0=c * 64, np_=64),
                           bass.AP(ps[3].vb(128, [(1, 128)]).tensor, c * 64 * 1024 + 128, [[1024, 64], [1, 128]]),
                           [ps[3]], [qt], eng="act")

            pend3 = []

            def flush():
                nonlocal pend1, pend2, pend3
                for f in pend3:
                    f()
                pend3 = pend2
                pend2 = []
                for (tt_, pa_) in pend1:
                    evac1(tt_, pa_)
                    pend2.append(lambda tt_=tt_: evac2(tt_))
                pend1 = []

            for t in range(16):
                own = t >= 8
                pA = ps[t % 3]
                ncol = 256 if own else 128
                nstep = 4 if own else 2
                per = 16 // nstep
                for st in range(nstep):
                    for kc in range(st * per, st * per + per):
                        mm(pA.v(0, [(ncol, 2), (1, ncol)]), hT.v(kc * 2048 + t * 128, [(1, 128)]), wrhs(kc, ncol),
                           kc == 0, kc == 15, [W0, W1] + hT.k(t), [pA])
                    if st == 0:
                        flush()
                    yield
                pend1.append((t, pA))
            wdone(wi0)
            wdone(wi1)
            flush()
            flush()
            flush()

        def diff_attn_gen(h):
            hb = h % 2
            kt, vaB, qt, gt = KT[hb], VA[hb], QT[hb], GT[hb]

            def Kfn(kb, G, q0, nq, c=None):
                if c is not None:
                    return (kt.v(kb * 128, [(1, 128)]), qt.v(G * 512 + c * 256 + q0, [(1, nq)]), [kt, qt])
                return (kt.v(kb * 128, [(1, 128)]), qt.v(G * 512, [(1, 512)]), [kt, qt])

            def Vfn(kb):
                return vaB.v(kb * 130, [(1, 129)]), [vaB]

            def fin_head(G):
                cp(Oev[0].v(), ps[6].v(0, [(1, 258)]), [ps[6]], [Oev[0]], eng="act")
                cp(Oev[1].v(), ps[7].v(0, [(1, 258)]), [ps[7]], [Oev[1]], eng="act")
                F0, F1, F2 = fin
                v3 = [(128, 2), (1, 128)]
                recip(sm.v(RC0, [(1, 2)]), Oev[0].v(128, [(129, 2)]), [Oev[0]], sm.k(RC0, RC0 + 1))
                recip(sm.v(RC1, [(1, 2)]), Oev[1].v(128, [(129, 2)]), [Oev[1]], sm.k(RC1, RC1 + 1))
                tt(sm.v(NL, [(1, 2)]), sm.v(RC1, [(1, 2)]), sm.v(NLAM, [(0, 2)]), ALU.mult,
                   sm.k(RC1, RC1 + 1, NLAM), sm.k(NL, NL + 1))
                tt(F0.v(0, v3), Oev[1].v(0, [(129, 2), (1, 128)]), sm.v(NL, [(1, 2), (0, 128)]), ALU.mult,
                   [Oev[1]] + sm.k(NL, NL + 1), [F0])
                tt(F1.v(0, v3), Oev[0].v(0, [(129, 2), (1, 128)]), sm.v(RC0, [(1, 2), (0, 128)]), ALU.mult,
                   [Oev[0]] + sm.k(RC0, RC0 + 1), [F1])
                tt(F1.v(), F1.v(), F0.v(), ALU.add, [F1, F0], [F1])
                for jj in range(2):
                    P.op("dve", lambda h_, o=F0.v(jj * 128, [(1, 128)]), a=F1.v(jj * 128, [(1, 128)]),
                         acc=sm.v(SSQ + jj, [(1, 1)]): h_.scalar_tensor_tensor(
                        out=o, in0=a, scalar=1.0, in1=a, op0=ALU.mult, op1=ALU.mult, accum_out=acc),
                        [F1], [F0] + sm.k(SSQ + jj))
                return lambda: fin_head2(G)

            def fin_head2(G):
                F0, F1, F2 = fin
                v3 = [(128, 2), (1, 128)]
                act(sm.v(LNQ, [(1, 2)]), sm.v(SSQ, [(1, 2)]), AF.Ln, sm.k(SSQ, SSQ + 1), sm.k(LNQ, LNQ + 1),
                    scale=1.0 / 128, bias=EPS)
                act(sm.v(RSQ, [(1, 2)]), sm.v(LNQ, [(1, 2)]), AF.Exp, sm.k(LNQ, LNQ + 1), sm.k(RSQ, RSQ + 1),
                    scale=-0.5, bias=float(np.log(1.0 - LAM_INIT)))
                tt(F2.v(0, v3), F1.v(0, v3), sm.v(RSQ, [(1, 2), (0, 128)]), ALU.mult, [F1] + sm.k(RSQ, RSQ + 1), [F2])
                tt(F2.v(), F2.v(), gt.v(2 * G * 128, [(1, 256)]), ALU.mult, [F2, gt], [F2])
                tt(dout01.v(0, v3), F2.v(0, v3), prm.v(PO_SUBW, [(0, 2), (1, 128)]), ALU.mult, [F2, prm], [dout01])

            def fin_tail(G, jj):
                if jj == 1:
                    return
                for j2 in range(2):
                    tr(ps[3].vb(512 + j2 * 128, [(1, 128)]), dout01.v(j2 * 128, [(1, 128)]), identb.v(), [dout01, identb],
                       [ps[3]])
                cp(mixT.v(h * 1024 + 2 * G * 128, [(1, 256)]), ps[3].vb(512, [(1, 256)]), [ps[3]],
                   mixT.k(h * 8 + 2 * G, h * 8 + 2 * G + 1), eng="act")

            return attention_gen(2, 8, True, 0.125, Kfn, Vfn, fin_head, fin_tail)

        def xattn_proj_gen(hp):
            Wq, wiq = wnext(22 + hp)
            Wg, wig = wnext(24 + hp)
            sl0, sl1 = wiq % 3, wig % 3
            pend = []

            def wrhs(kc):
                return bass.AP(wbig, sl0 * 4096 + kc * 256, [[3 * 4096, 128], [(sl1 - sl0) * 4096, 2], [1, 256]])

            def evac(j, pA):
                cp(qx.v(), pA.v(0, [(1, 256)]), [pA], [qx], eng="act")
                for hd in range(2):
                    tr(ps[3].vb(hd * 128, [(1, 128)]), qx.v(hd * 128, [(1, 128)]), identb.v(), [qx, identb], [ps[3]])
                    cp(XQ[hp][hd].v(j * 128, [(1, 128)]), ps[3].vb(hd * 128, [(1, 128)]), [ps[3]], [XQ[hp][hd]], eng="act")
                    silu_exp(XG[hp][hd].v(j * 128, [(1, 128)]), XG[hp][hd], pA.v(256 + hd * 128, [(1, 128)]), pA)

            for j in range(8):
                t = 8 + j
                pA = ps[j % 3]
                for st in range(4):
                    for kc in range(st * 4, st * 4 + 4):
                        mm(pA.v(0, [(256, 2), (1, 256)]), hT.v(kc * 2048 + t * 128, [(1, 128)]), wrhs(kc),
                           kc == 0, kc == 15, [Wq, Wg] + hT.k(t), [pA])
                    if st == 0:
                        for (j_, p_) in pend:
                            evac(j_, p_)
                        pend = []
                    yield
                pend.append((j, pA))
            wdone(wiq)
            wdone(wig)
            for (j_, p_) in pend:
                evac(j_, p_)

        def xattn_attn_gen(hp, hd):
            hx = 2 * hp + hd
            xq, xg = XQ[hp][hd], XG[hp][hd]

            def KfnX(kb, G, q0, nq, c=None):
                return (mkT.v(hx * 256 + kb * 128, [(1, 128)]), xq.v(G * 256, [(1, 256)]), [mkT, xq])

            def VfnX(kb):
                return mvA.v((kb * 4 + hx) * 130, [(1, 129)]), [mvA]

            def fin_head_x(G):
                cp(Oev[0].v(), ps[6].v(0, [(1, 258)]), [ps[6]], [Oev[0]], eng="act")
                for jj in range(2):
                    j = 2 * G + jj
                    recip(sm.v(RC0 + jj, [(1, 1)]), Oev[0].v(jj * 129 + 128, [(1, 1)]), [Oev[0]], sm.k(RC0 + jj))
                    stt(dout01.v(jj * 128, [(1, 128)]), Oev[0].v(jj * 129, [(1, 128)]), sm.v(RC0 + jj, [(1, 1)]),
                        xg.v(j * 128, [(1, 128)]), ALU.mult, ALU.mult, [Oev[0], xg] + sm.k(RC0 + jj), [dout01])

            def fin_tail_x(G, jj):
                if jj == 1:
                    return
                for j2 in range(2):
                    tr(ps[3].vb(512 + j2 * 128, [(1, 128)]), dout01.v(j2 * 128, [(1, 128)]), identb.v(), [dout01, identb],
                       [ps[3]])
                cp(mixT.v((12 + hx) * 1024 + 2 * G * 128, [(1, 256)]), ps[3].vb(512, [(1, 256)]), [ps[3]],
                   mixT.k((12 + hx) * 8 + 2 * G, (12 + hx) * 8 + 2 * G + 1), eng="act")

            return attention_gen(1, 2, False, float(128 ** -0.5), KfnX, VfnX, fin_head_x, fin_tail_x)

        def run_with_filler(A, B):
            for _ in A:
                if B is not None:
                    try:
                        next(B)
                    except StopIteration:
                        B = None
            return B

        run_gen(diff_proj_gen(0))
        stage_end(3.3)
        for h in range(8):
            B = diff_proj_gen(h + 1) if h < 7 else xattn_proj_gen(0)
            B = run_with_filler(diff_attn_gen(h), B)
            if B is not None:
                run_gen(B)
            if h == 0:
                stage_end(3.6)

        stage_end(4)
        B = xattn_proj_gen(1)
        for hd in range(2):
            B = run_with_filler(xattn_attn_gen(0, hd), B)
        if B is not None:
            run_gen(B)
        for hd in range(2):
            run_gen(xattn_attn_gen(1, hd))
        for ent in carry:
            ent[1]()
        del carry[:]

        stage_end(5)
        dma(bigrow.v(), rows_d[2:3, :].partition_broadcast(128), [], [bigrow])
        for n in range(8):
            W, wi = wnext(30 + n)
            for j in range(8):
                pb = ps[(n * 8 + j) % 4]
                for fc in range(16):
                    mm(pb.v(0, [(1, 256)]), mixT.v(fc * 1024 + j * 128, [(1, 128)]), W.v(fc * 256, [(1, 256)]),
                       fc == 0, fc == 15, [W] + mixT.k(fc * 8 + j), [pb])
                cp(YB[j].v(n * 256, [(1, 256)]), pb.v(0, [(1, 256)]), [pb], [YB[j]], eng=("act" if j % 2 else "dve"))
            wdone(wi)
        def fstats(j):
            xs = XS[j % 3]
            c = 20 + j
            dma(xs.v(), x_d[(8 + j) * 128:(9 + j) * 128, :], [], [xs])
            act(junk.v(), YB[j].v(), AF.Square, [YB[j]], [junk] + ss.k(c), accum=ss.v(c, [(1, 1)]))
            act(lnv.v(c, [(1, 1)]), ss.v(c, [(1, 1)]), AF.Ln, ss.k(c), lnv.k(c), scale=1.0 / 2048, bias=EPS)
            act(rstd.v(c, [(1, 1)]), lnv.v(c, [(1, 1)]), AF.Exp, lnv.k(c), rstd.k(c), scale=-0.5)

        def frest(j):
            xs = XS[j % 3]
            c = 20 + j
            stt(YB[j].v(), YB[j].v(), rstd.v(c, [(1, 1)]), bigrow.v(), ALU.mult, ALU.mult, [YB[j], bigrow] + rstd.k(c), [YB[j]])
            tt(xs.v(), xs.v(), YB[j].v(), ALU.add, [xs, YB[j]], [xs])
            stores.append(dma(out_d[j * 128:(j + 1) * 128, :], xs.v(), [xs], ["out%d" % j], semkey=xs.name + "_st"))

        fstats(0)
        for j in range(8):
            if j + 1 < 8:
                fstats(j + 1)
            frest(j)
    except _Stop:
        pass
    if dbg:
        stores.append(dma(dbg_d, mixT.v(), [mixT], ["dbgout"], semkey="dbgout"))
        if stage < 4:
            stores.append(dma(dbg2_d, hT.v(), [hT], ["dbgout2"], semkey="dbgout2"))
    cnt = P.emit(final_waits=stores)
    build.names = P.names
    return nc, cnt


def _hg_cols():
    o_q, o_k, o_v, o_g = 0, 1024, 2048, 3072
    o_z, o_xbc, o_dt, o_xq, o_xg = 4096, 4608, 5632, 5640, 6152
    ar = np.arange
    groups = []
    for h in range(8):
        groups.append(("in", np.concatenate([o_k + h * 128 + ar(128), o_q + h * 128 + ar(128)])))
        groups.append(("in", np.concatenate([o_v + h * 128 + ar(128), o_g + h * 128 + ar(128)])))
    for i in range(4):
        groups.append(("in", o_xbc + i * 256 + ar(256)))
    for i in range(2):
        groups.append(("in", o_z + i * 256 + ar(256)))
    for i in range(2):
        groups.append(("in", o_xq + i * 256 + ar(256)))
    for i in range(2):
        groups.append(("in", o_xg + i * 256 + ar(256)))
    for i in range(2):
        groups.append(("kv", i * 256 + ar(256)))
    for i in range(2):
        groups.append(("kv", 512 + i * 256 + ar(256)))
    for i in range(8):
        groups.append(("out", i * 256 + ar(256)))
    return groups, o_dt


_NC_CACHE = {}


def _prep(inputs):
    f32 = np.float32
    w_in = np.asarray(inputs["w_in"], f32)[0]
    w_kv = np.asarray(inputs["w_mem_kv"], f32)[0]
    w_out = np.asarray(inputs["w_out"], f32)[0]
    groups, o_dt = _hg_cols()
    wg = np.empty((NHG, 128, 16, 256), f32)
    src = {"in": w_in, "kv": w_kv, "out": w_out}
    for i, (which, cols) in enumerate(groups):
        wg[i] = src[which][:, cols].reshape(16, 128, 256).transpose(1, 0, 2)
    wg = wg.reshape(NHG, 128, 4096)
    wdt = np.ascontiguousarray(w_in[:, o_dt:o_dt + 8].reshape(16, 128, 8).transpose(1, 0, 2)).reshape(128, 128)
    prm = np.zeros((128, NPRM), f32)
    bc = lambda v: np.broadcast_to(np.asarray(v, f32).reshape(1, -1), (128, np.asarray(v).size))
    prm[:, PO_SSDW:PO_SSDW + 512] = bc(inputs["ssd_norm_w"][0])
    prm[:, PO_SUBW:PO_SUBW + 128] = bc(inputs["diff_subln_w"][0])
    prm[:, PO_DSK:PO_DSK + 8] = bc(inputs["d_skip"][0])
    prm[:, PO_DTB:PO_DTB + 8] = bc(inputs["dt_bias"][0])
    prm[:, PO_ALOG:PO_ALOG + 8] = bc(inputs["a_log"][0])
    for i, nm in enumerate(("lambda_q1", "lambda_k1", "lambda_q2", "lambda_k2")):
        prm[:, PO_LAM + i * 64:PO_LAM + (i + 1) * 64] = bc(inputs[nm][0])
    cw = np.asarray(inputs["conv_w"], f32)[0]
    prm[:, PO_CW:PO_CW + 32] = cw.reshape(4, 8, 128).transpose(2, 1, 0).reshape(128, 32)
    prm[:, PO_CB:PO_CB + 8] = np.asarray(inputs["conv_b"], f32)[0].reshape(8, 128).T
    cf = (1.0 / (10000.0 ** (np.arange(0, 64, 2, dtype=np.float64) / 64.0))) / (2.0 * np.pi)
    c_hi = cf.astype(f32)
    c_lo = (cf - c_hi.astype(np.float64)).astype(f32)
    prm[:, PO_INVF:PO_INVF + 32] = bc(c_hi)
    prm[:, PO_INVL:PO_INVL + 32] = bc(c_lo)
    rows = np.stack([np.asarray(inputs["pre_norm_w"], f32)[0], np.asarray(inputs["mem_norm_w"], f32)[0],
                     np.asarray(inputs["post_norm_w"], f32)[0]])
    cst = np.zeros((128, 512), f32)
    s = np.arange(128)
    cst[:, C_ID:C_ID + 128] = np.eye(128, dtype=f32)
    tri = (s[:, None] <= s[None, :]).astype(f32)
    cst[:, C_TRI:C_TRI + 128] = tri
    cst[:, C_NEG:C_NEG + 128] = (tri - 1.0) * 30000.0
    cst[:, C_ONE:C_ONE + 128] = 1.0
    x = np.asarray(inputs["x"], f32)
    mem = np.asarray(inputs["mem"], f32)
    pos = np.asarray(inputs["positions"], np.int32)
    in_maps = []
    for b in range(4):
        for half in range(2):
            if half == 0:
                xt = np.concatenate([np.zeros((1024, 2048), f32), x[b, :1024]], axis=0)
                pp = np.concatenate([np.zeros(1024, np.int32), pos[b, :1024]])
            else:
                xt = x[b]
                pp = pos[b]
            in_maps.append({
                "x_tok": np.ascontiguousarray(xt), "memx": np.ascontiguousarray(mem[b]),
                "pos": np.ascontiguousarray(pp.reshape(16, 128).T), "flag": np.full((128, 1), float(half), f32),
                "wg": wg, "wdt": wdt, "prm": prm, "rows": rows, "cst": cst})
    return in_maps


def kernel(**inputs):
    if "nc" not in _NC_CACHE:
        _NC_CACHE["nc"] = build()[0]
    nc = _NC_CACHE["nc"]
    in_maps = _prep(inputs)
    res = run_bass_kernel_spmd(nc, in_maps, core_ids=list(range(8)))
    out = np.empty((4, 2048, 2048), np.float32)
    for b in range(4):
        for half in range(2):
            out[b, half * 1024:(half + 1) * 1024] = res.results[2 * b + half]["out"]
    return out
```

```python
import numpy as np
import concourse.bass as bass
import concourse.mybir as mybir
from concourse.bass_utils import run_bass_kernel_spmd

F32 = mybir.dt.float32
BF16 = mybir.dt.bfloat16
I32 = mybir.dt.int32
U8 = mybir.dt.uint8
ALU = mybir.AluOpType
AF = mybir.ActivationFunctionType
AX = mybir.AxisListType

ENGS = ("pe", "act", "dve", "pool", "sp")
EPS = 1e-6
TWO_PI = float(2 * np.pi)


class Buf:
    _n = 0

    def __init__(self, P, name, shape, dtype, space="sbuf", nslot=1, at=None, keys=None):
        Buf._n += 1
        self.name = f"{name}_{Buf._n}"
        self.shape = list(shape)
        self.dtype = dtype
        self.nslot = nslot
        if space == "psum":
            self.t = P.nc.alloc_psum_tensor(self.name, self.shape, dtype)
            P.psum_names.add(self.name)
        elif at is not None:
            self.t = P.nc.alloc_sbuf_tensor_at(self.name, self.shape, dtype, offset=at)
        else:
            self.t = P.nc.alloc_sbuf_tensor(self.name, self.shape, dtype)
        self.rowlen = int(np.prod(self.shape[1:]))
        self.keys = keys if keys is not None else [(self.name, i) for i in range(nslot)]

    def k(self, *slots):
        if not slots:
            return list(self.keys)
        return [self.keys[s] for s in slots]

    def v(self, off=0, dims=None, p0=0, np_=None):
        if np_ is None:
            np_ = self.shape[0] - p0
        if dims is None:
            dims = [(1, self.rowlen - off)]
        ap = [[self.rowlen, np_]] + [[s, n] for (s, n) in dims]
        return bass.AP(self.t, p0 * self.rowlen + off, ap)

    def vb(self, off=0, dims=None, np_=None):
        rl = self.rowlen * 2
        if np_ is None:
            np_ = self.shape[0]
        if dims is None:
            dims = [(1, rl - off)]
        a = bass.AP(self.t, 0, [[self.rowlen, self.shape[0]], [1, self.rowlen]]).bitcast(BF16)
        t = a.tensor
        ap = [[rl, np_]] + [[s, n] for (s, n) in dims]
        return bass.AP(t, off, ap)


class Op:
    __slots__ = ("eng", "fn", "waits", "idx", "signal", "semval", "dma_sem", "dma_val", "is_dma", "tag")


class Prog:
    def __init__(self, nc):
        self.nc = nc
        self.ops = {e: [] for e in ENGS}
        self.last_w = {}
        self.readers = {}
        self.dma_sems = {}
        self.seen = {e: {} for e in ENGS}
        self.psum_names = set()
        self.names = {}

    def _keys(self, lst):
        out = []
        for x in lst:
            if isinstance(x, Buf):
                out.extend(x.keys)
            elif isinstance(x, list):
                out.extend(x)
            else:
                out.append(x)
        return out

    def op(self, eng, fn, reads=(), writes=(), dma=False, semkey=None):
        o = Op()
        import sys as _sys
        f_ = _sys._getframe(2)
        o.tag = "%d/%d" % (f_.f_lineno, f_.f_back.f_lineno if f_.f_back else 0)
        o.eng, o.fn, o.is_dma, o.signal, o.semval, o.dma_sem = eng, fn, dma, False, None, None
        o.idx = len(self.ops[eng])
        rk = self._keys(reads)
        wk = self._keys(writes)
        for k in rk:
            if isinstance(k, tuple) and k[0] in self.psum_names and k not in wk:
                wk.append(k)
        deps = []
        for k in rk:
            w = self.last_w.get(k)
            if w is not None:
                deps.append((w, "raw"))
        for k in wk:
            w = self.last_w.get(k)
            if w is not None:
                deps.append((w, "waw"))
            for r in self.readers.get(k, ()):
                deps.append((r, "war"))
        waits = {}
        for (d, kind) in deps:
            if d is o:
                continue
            if d.is_dma:
                waits[("dma", id(d))] = d
                continue
            if d.eng == eng and not dma:
                if eng == "pe":
                    continue
            prev = waits.get(("eng", d.eng))
            if prev is None or prev.idx < d.idx:
                waits[("eng", d.eng)] = d
        final = []
        for key, d in waits.items():
            if key[0] == "eng":
                s = self.seen[eng].get(d.eng, -1)
                if d.idx <= s:
                    continue
                self.seen[eng][d.eng] = d.idx
                d.signal = True
            else:
                if ("dmaseen", id(d)) in self.seen[eng]:
                    continue
                self.seen[eng][("dmaseen", id(d))] = True
            final.append(d)
        o.waits = final
        if dma:
            bkey = semkey
            if bkey is None:
                for x in list(writes) + list(reads):
                    if isinstance(x, Buf):
                        bkey = x.name
                        break
            assert bkey is not None
            ent = self.dma_sems.get(bkey)
            if ent is None:
                ent = [self.nc.alloc_semaphore("d_" + str(bkey)), 0]
                self.dma_sems[bkey] = ent
            ent[1] += 16
            o.dma_sem, o.dma_val = ent[0], ent[1]
        for k in rk:
            self.readers.setdefault(k, []).append(o)
        for k in wk:
            self.last_w[k] = o
            self.readers[k] = []
        self.ops[eng].append(o)
        return o

    def emit(self, final_waits=()):
        nc = self.nc
        self.esem = {e: nc.alloc_semaphore("e_" + e) for e in ENGS}
        for d in final_waits:
            if not d.is_dma:
                d.signal = True
        for e in ENGS:
            c = 0
            for o in self.ops[e]:
                if o.signal and not o.is_dma:
                    c += 1
                    o.semval = c
        prog = self

        def run(e, h):
            for o in prog.ops[e]:
                for d in o.waits:
                    if d.is_dma:
                        h.wait_ge(d.dma_sem, d.dma_val)
                    else:
                        h.wait_ge(prog.esem[d.eng], d.semval)
                ins = o.fn(h)
                try:
                    prog.names[ins.ins.name] = o.tag
                except Exception:
                    try:
                        prog.names[ins.name] = o.tag
                    except Exception:
                        pass
                if o.is_dma:
                    ins.then_inc(o.dma_sem, 16)
                elif o.signal:
                    ins.then_inc(prog.esem[e], 1)
            if e == "sp":
                for d in final_waits:
                    if d.is_dma:
                        h.wait_ge(d.dma_sem, d.dma_val)
                    else:
                        h.wait_ge(prog.esem[d.eng], d.semval)

        with nc.Block() as block:
            @block.tensor
            def _(h):
                run("pe", h)

            @block.scalar
            def _(h):
                run("act", h)

            @block.vector
            def _(h):
                run("dve", h)

            @block.gpsimd
            def _(h):
                run("pool", h)

            @block.sync
            def _(h):
                run("sp", h)
        return {e: len(self.ops[e]) for e in ENGS}


NHG = 38
PO_SSDW = 0
PO_SUBW = 512
PO_DSK = 640
PO_DTB = 648
PO_ALOG = 656
PO_LAM = 664
PO_CW = 920
PO_CB = 952
PO_INVF = 960
PO_INVL = 992
NPRM = 1024
C_ID, C_TRI, C_NEG, C_ONE = 0, 128, 256, 384

WSEQ = [26, 27, 28, 29, 16, 17, 18, 19, 20, 21] + list(range(16)) + [22, 24, 23, 25] + list(range(30, 38))


class _Stop(Exception):
    pass


def build(dbg=False, stage=99):
    nc = bass.Bass("TRN2", target_bir_lowering=False)
    P = Prog(nc)
    stores = []

    def stage_end(n):
        if stage == n:
            raise _Stop()

    def dram(name, shape, dt, kind="ExternalInput"):
        return nc.dram_tensor(name, shape, dt, kind=kind).ap()

    x_d = dram("x_tok", [2048, 2048], F32)
    mem_d = dram("memx", [256, 2048], F32)
    pos_d = dram("pos", [128, 16], I32)
    flag_d = dram("flag", [128, 1], F32)
    wg_d = dram("wg", [NHG, 128, 4096], F32)
    wdt_d = dram("wdt", [128, 128], F32)
    prm_d = dram("prm", [128, NPRM], F32)
    rows_d = dram("rows", [3, 2048], F32)
    cst_d = dram("cst", [128, 512], F32)
    out_d = dram("out", [1024, 2048], F32, kind="ExternalOutput")
    if dbg:
        dbg_d = dram("dbg", [128, 16 * 1024], BF16, kind="ExternalOutput")
        dbg2_d = dram("dbg2", [128, 16 * 2048], BF16, kind="ExternalOutput")

    def mm(out, lhsT, rhs, start, stop, R, W, skip=False):
        return P.op("pe", lambda h: h.matmul(out, lhsT=lhsT, rhs=rhs, start=start, stop=stop, skip_group_check=skip), R, W)

    def tr(out, in_, ident, R, W):
        return P.op("pe", lambda h: h.transpose(out=out, in_=in_, identity=ident), R, W)

    def act(out, in_, func, R, W, bias=None, scale=None, accum=None):
        kw = {}
        if bias is not None:
            kw["bias"] = bias
        if scale is not None:
            kw["scale"] = scale
        if accum is not None:
            kw["accum_out"] = accum
        return P.op("act", lambda h: h.activation(out=out, in_=in_, func=func, **kw), R, W)

    def tt(out, in0, in1, op, R, W, eng="dve"):
        return P.op(eng, lambda h: h.tensor_tensor(out=out, in0=in0, in1=in1, op=op), R, W)

    def ts(out, in0, s1, s2, op0, op1, R, W, eng="dve"):
        if op1 is None:
            return P.op(eng, lambda h: h.tensor_scalar(out=out, in0=in0, scalar1=s1, scalar2=None, op0=op0), R, W)
        return P.op(eng, lambda h: h.tensor_scalar(out=out, in0=in0, scalar1=s1, scalar2=s2, op0=op0, op1=op1), R, W)

    def stt(out, in0, scalar, in1, op0, op1, R, W):
        return P.op("dve", lambda h: h.scalar_tensor_tensor(out=out, in0=in0, scalar=scalar, in1=in1, op0=op0, op1=op1), R, W)

    def cp(out, in_, R, W, eng="dve"):
        if eng == "act":
            return P.op("act", lambda h: h.copy(out, in_), R, W)
        return P.op(eng, lambda h: h.tensor_copy(out, in_), R, W)

    def dma(out, in_, R, W, eng="sp", semkey=None):
        return P.op(eng, lambda h: h.dma_start(out=out, in_=in_), R, W, dma=True, semkey=semkey)

    def memset(ap, val, W, eng="dve"):
        return P.op(eng, lambda h: h.memset(ap, val), (), W)

    def recip(out, in_, R, W):
        return P.op("dve", lambda h: h.reciprocal(out, in_), R, W)

    hT = Buf(P, "hT", [128, 16 * 2048], BF16, nslot=16)
    mixT = Buf(P, "mixT", [128, 16 * 1024], BF16, nslot=128)
    wbig = nc.alloc_sbuf_tensor("wbig", [128, 3 * 4096], BF16)

    class SubBuf(Buf):
        def __init__(self, name, parent_t, parent_rowlen, col0, ncol):
            Buf._n += 1
            self.name = f"{name}_{Buf._n}"
            self.shape = [128, ncol]
            self.t = parent_t
            self.rowlen = ncol
            self.prow = parent_rowlen
            self.col0 = col0
            self.nslot = 1
            self.keys = [(self.name, 0)]

        def v(self, off=0, dims=None, p0=0, np_=None):
            if np_ is None:
                np_ = 128 - p0
            if dims is None:
                dims = [(1, self.rowlen - off)]
            ap = [[self.prow, np_]] + [[s_, n_] for (s_, n_) in dims]
            return bass.AP(self.t, p0 * self.prow + self.col0 + off, ap)

    wr = [SubBuf(f"wr{i}", wbig, 3 * 4096, i * 4096, 4096) for i in range(3)]
    ARENA = 59392
    arena = nc.alloc_sbuf_tensor("arena", [128, ARENA], U8)
    abase = nc.lookup_mloc(arena).addr
    hbase = nc.lookup_mloc(hT.t).addr

    def aview(name, shape, dt, off, nbytes):
        keys = [("arena", i) for i in range(off // 1024, (off + nbytes - 1) // 1024 + 1)]
        return Buf(P, name, shape, dt, at=abase + off, keys=keys)

    XS = [aview(f"xs{i}", [128, 2048], F32, i * 8192, 8192) for i in range(2)] + [aview("xs2", [128, 2048], F32, 45056, 8192)]
    XN = [aview(f"xn{i}", [128, 2048], BF16, 16384 + i * 4096, 4096) for i in range(2)]
    bigrow = aview("bigrow", [128, 2048], F32, 24576, 8192)
    memT = aview("memT", [128, 16 * 256], BF16, 32768, 8192)
    junk = aview("junk", [128, 2048], BF16, 40960, 4096)
    tmpA = [aview(f"tmpA{i}", [128, 512], F32, 45056 + i * 2048, 2048) for i in range(4)]
    xbcF = aview("xbcF", [128, 8 * 2048], BF16, 0, 32768)
    RP = 256
    raw = Buf(P, "raw", [128, RP + 2048], F32, at=abase + 32768)
    raw.keys = [("arena", 32 + i) for i in range(0, 9)]
    rawk = lambda G: [("arena", 33 + 2 * G), ("arena", 34 + 2 * G)]
    rawh = lambda G: [("arena", 32 + 2 * G)] + rawk(G)
    sA = 41984
    accs = [aview(f"acc{i}", [128, 512], F32, sA + i * 2048, 2048) for i in range(2)]
    ytm = [aview(f"ytm{i}", [128, 512], F32, sA + 4096 + i * 2048, 2048) for i in range(2)]
    KT = [aview(f"KT{i}", [128, 2048], BF16, i * 4096, 4096) for i in range(2)]
    VA = [aview(f"VA{i}", [128, 16 * 130], BF16, 16384 + i * 5120, 4160) for i in range(2)]
    QT = [aview(f"QT{i}", [128, 2048], BF16, 26624 + i * 4096, 4096) for i in range(2)]
    GT = [aview(f"GT{i}", [128, 8 * 128], BF16, 34816 + i * 2048, 2048) for i in range(2)]
    PT = [aview(f"PT{i}", [128, 512], BF16, 38912 + i * 1024, 1024) for i in range(4)]
    ropeA = aview("ropeA", [128, 256], F32, 43008, 1024)
    ropeB = aview("ropeB", [128, 256], F32, 44032, 1024)
    kr01s = [aview("kr01a", [128, 256], BF16, 45056, 512), aview("kr01b", [128, 256], BF16, 46080, 512)]
    fin = [aview(f"fin{i}", [128, 256], F32, 47104 + i * 1024, 1024) for i in range(3)]
    dout01 = aview("dout01", [128, 256], BF16, 50176, 512)
    qx = aview("qx", [128, 256], BF16, 52224, 512)
    Oev = [aview(f"oev{i}", [128, 258], F32, 53248 + i * 2048, 1032) for i in range(2)]
    sg = [aview(f"sg{i}", [128, 128], F32, 57344 + i * 1024, 512) for i in range(2)]
    xoff = [8192, 0]
    XQ = [[aview(f"xq{hp}{hd}", [128, 1024], BF16, xoff[hp] + hd * 2048, 2048) for hd in range(2)] for hp in range(2)]
    XG = [[aview(f"xg{hp}{hd}", [128, 1024], BF16, xoff[hp] + 4096 + hd * 2048, 2048) for hd in range(2)] for hp in range(2)]
    YB = [Buf(P, f"yb{j}", [128, 2048], F32, at=hbase + j * 8192) for j in range(8)]

    cst = Buf(P, "cst", [128, 512], F32)
    identb = Buf(P, "identb", [128, 128], BF16)
    trib = Buf(P, "trib", [128, 128], BF16)
    negmb = Buf(P, "negmb", [128, 128], BF16)
    prm = Buf(P, "prm", [128, NPRM], F32)
    cosb = Buf(P, "cosb", [128, 512], F32)
    sinb = Buf(P, "sinb", [128, 512], F32)
    flag = Buf(P, "flag", [128, 1], F32)
    posi = Buf(P, "posi", [128, 16], I32)
    posf = Buf(P, "posf", [128, 16], F32)
    ki = Buf(P, "ki", [128, 512], I32)
    sm = Buf(P, "sm", [128, 64], F32, nslot=64)
    ss = Buf(P, "ss", [128, 32], F32, nslot=32)
    lnv = Buf(P, "lnv", [128, 32], F32, nslot=32)
    rstd = Buf(P, "rstd", [128, 32], F32, nslot=32)
    mkT = Buf(P, "mkT", [128, 4 * 256], BF16)
    mvA = Buf(P, "mvA", [128, 2 * 4 * 130], BF16)
    wdtf = Buf(P, "wdtf", [128, 128], F32)
    wdtb = Buf(P, "wdtb", [128, 128], BF16)
    dtb = Buf(P, "dtb", [128, 128], F32)
    adt = Buf(P, "adt", [128, 128], F32)
    Ahat = Buf(P, "Ahat", [128, 8], F32)
    Sst = Buf(P, "Sst", [128, 512], F32)
    Sbf = Buf(P, "Sbf", [128, 512], BF16)
    acs = Buf(P, "acs", [128, 16], F32)
    sc8 = [Buf(P, f"sc8_{i}", [128, 8], F32) for i in range(6)]
    xsT = Buf(P, "xsT", [128, 512], BF16)
    BTb = Buf(P, "BTb", [128, 256], BF16)
    Xb = Buf(P, "Xb", [128, 512], BF16)
    Xd = Buf(P, "Xd", [128, 512], BF16)
    zs = Buf(P, "zs", [128, 512], BF16)
    sob = Buf(P, "sob", [128, 512], BF16)
    ps = [Buf(P, f"ps{i}", [128, 512], F32, space="psum") for i in range(8)]

    try:
        dma(cst.v(), cst_d, [], [cst])
        dma(prm.v(), prm_d, [], [prm])
        dma(flag.v(), flag_d, [], [flag])
        dma(posi.v(), pos_d, [], [posi])
        dma(wdtf.v(), wdt_d, [], [wdtf])
        dma(bigrow.v(), rows_d[0:1, :].partition_broadcast(128), [], [bigrow])
        cp(identb.v(), cst.v(C_ID, [(1, 128)]), [cst], [identb])
        cp(trib.v(), cst.v(C_TRI, [(1, 128)]), [cst], [trib])
        cp(negmb.v(), cst.v(C_NEG, [(1, 128)]), [cst], [negmb])
        cp(wdtb.v(), wdtf.v(), [wdtf], [wdtb])
        cp(posf.v(), posi.v(), [posi], [posf])
        u0, u1, u2, u3 = tmpA
        tt(u0.v(0, [(32, 16), (1, 32)]), posf.v(0, [(1, 16), (0, 32)]), prm.v(PO_INVF, [(0, 16), (1, 32)]), ALU.mult,
           [posf, prm], [u0])
        tt(u1.v(0, [(32, 16), (1, 32)]), posf.v(0, [(1, 16), (0, 32)]), prm.v(PO_INVL, [(0, 16), (1, 32)]), ALU.mult,
           [posf, prm], [u1])
        tt(u0.v(), u0.v(), u1.v(), ALU.add, [u0, u1], [u0])
        for (shift, tab) in ((0.0, sinb), (0.25, cosb)):
            ts(u1.v(), u0.v(), shift, None, ALU.add, None, [u0], [u1])
            cp(ki.v(), u1.v(), [u1], [ki])
            cp(u2.v(), ki.v(), [ki], [u2])
            tt(u3.v(), u1.v(), u2.v(), ALU.subtract, [u1, u2], [u3])
            stt(u2.v(), u3.v(), 0.5, u3.v(), ALU.is_gt, ALU.subtract, [u3], [u2])
            act(tab.v(), u2.v(), AF.Sin, [u2], [tab], scale=-TWO_PI)
        LAM_INIT = 0.2
        for i in range(2):
            tt(u1.v(0, [(1, 64)]), prm.v(PO_LAM + i * 128, [(1, 64)]), prm.v(PO_LAM + i * 128 + 64, [(1, 64)]), ALU.mult,
               [prm], [u1])
            P.op("dve", lambda h, o=sm.v(i, [(1, 1)]), a=u1.v(0, [(1, 64)]): h.tensor_reduce(out=o, in_=a, axis=AX.X, op=ALU.add),
                 [u1], sm.k(i))
            act(sm.v(2 + i, [(1, 1)]), sm.v(i, [(1, 1)]), AF.Exp, sm.k(i), sm.k(2 + i))
        tt(sm.v(4, [(1, 1)]), sm.v(3, [(1, 1)]), sm.v(2, [(1, 1)]), ALU.subtract, sm.k(2, 3), sm.k(4))
        ts(sm.v(4, [(1, 1)]), sm.v(4, [(1, 1)]), -LAM_INIT, None, ALU.add, None, sm.k(4), sm.k(4))
        NLAM = 4
        act(Ahat.v(), prm.v(PO_ALOG, [(1, 8)]), AF.Exp, [prm], [Ahat])
        ts(Ahat.v(), Ahat.v(), -1.0, None, ALU.mult, None, [Ahat], [Ahat])

        wstate = {"n": 0}

        def wissue(i, after=()):
            if i < len(WSEQ):
                b = wr[i % 3]
                dma(b.v(), wg_d[WSEQ[i]], list(after), [b], eng="pool")

        def wnext(expect):
            i = wstate["n"]
            assert WSEQ[i] == expect, (i, WSEQ[i], expect)
            wstate["n"] = i + 1
            return wr[i % 3], i

        def wdone(i):
            wissue(i + 3)

        wstart = {"done": False}

        def norm_tiles(src_d, ntile, dstbuf, dst_tok_stride, slot_of, col0):
            def stats(t):
                xs, xn = XS[t % 3], XN[t % 2]
                c = col0 + t
                dma(xs.v(), src_d[t * 128:(t + 1) * 128, :], [], [xs])
                act(junk.v(), xs.v(), AF.Square, [xs], [junk] + ss.k(c), accum=ss.v(c, [(1, 1)]))
                act(lnv.v(c, [(1, 1)]), ss.v(c, [(1, 1)]), AF.Ln, ss.k(c), lnv.k(c), scale=1.0 / 2048, bias=EPS)
                act(rstd.v(c, [(1, 1)]), lnv.v(c, [(1, 1)]), AF.Exp, lnv.k(c), rstd.k(c), scale=-0.5)

            def rest(t):
                xs, xn = XS[t % 3], XN[t % 2]
                c = col0 + t
                stt(xn.v(), xs.v(), rstd.v(c, [(1, 1)]), bigrow.v(), ALU.mult, ALU.mult, [xs, bigrow] + rstd.k(c), [xn])
                for q in range(4):
                    pb = ps[(t * 4 + q) % 4]
                    for i in range(4):
                        kc = q * 4 + i
                        tr(pb.vb(i * 128, [(1, 128)]), xn.v(kc * 128, [(1, 128)]), identb.v(), [xn, identb], [pb])
                    o = dstbuf.v(q * 4 * dst_tok_stride + t * 128, [(dst_tok_stride, 4), (1, 128)])
                    cp(o, pb.vb(0, [(128, 4), (1, 128)]), [pb], slot_of(t), eng=("act" if q % 2 else "dve"))

            stats(0)
            if ntile > 1:
                stats(1)
            for t in range(ntile):
                if t + 2 < ntile:
                    stats(t + 2)
                rest(t)
                if t == 2 and not wstart["done"]:
                    wstart["done"] = True
                    for i_ in range(3):
                        wissue(i_, after=[XN[t % 2]])

        norm_tiles(x_d, 16, hT, 2048, lambda t: hT.k(t), 0)
        dma(bigrow.v(), rows_d[1:2, :].partition_broadcast(128), [], [bigrow])
        norm_tiles(mem_d, 2, memT, 256, lambda t: [memT], 16)

        stage_end(1)
        memset(mvA.v(), 1.0, [mvA])
        stage_end(1.2)
        pcount = {"n": 0}

        def nextps(lo=0, hi=2):
            pcount["n"] += 1
            return ps[lo + pcount["n"] % (hi - lo)]

        for hp in range(2):
            W, wi = wnext(26 + hp)
            for hd in range(2):
                hx = 2 * hp + hd
                pb = nextps()
                for kc in range(16):
                    mm(pb.v(0, [(1, 256)]), W.v(kc * 256 + hd * 128, [(1, 128)]), memT.v(kc * 256, [(1, 256)]),
                       kc == 0, kc == 15, [W, memT], [pb])
                cp(mkT.v(hx * 256, [(1, 256)]), pb.v(0, [(1, 256)]), [pb], [mkT], eng="act")
            wdone(wi)
        stage_end(1.5)
        for hp in range(2):
            W, wi = wnext(28 + hp)
            for mt in range(2):
                pb = nextps()
                for kc in range(16):
                    mm(pb.v(0, [(1, 256)]), memT.v(kc * 256 + mt * 128, [(1, 128)]), W.v(kc * 256, [(1, 256)]),
                       kc == 0, kc == 15, [W, memT], [pb])
                o = mvA.v((mt * 4 + 2 * hp) * 130, [(130, 2), (1, 128)])
                cp(o, pb.v(0, [(128, 2), (1, 128)]), [pb], [mvA], eng="act")
            wdone(wi)

        stage_end(2)
        memset(raw.v(RP - 4, [(1, 4)]), 0.0, [("arena", 32)])
        memset(Sst.v(), 0.0, [Sst])
        memset(Sbf.v(), 0.0, [Sbf])
        for t in range(16):
            for kc in range(16):
                mm(ps[7].v(t * 8, [(1, 8)]), hT.v(kc * 2048 + t * 128, [(1, 128)]), wdtb.v(kc * 8, [(1, 8)]),
                   kc == 0, kc == 15, [wdtb] + hT.k(t), [ps[7]])
        tt(dtb.v(0, [(8, 16), (1, 8)]), ps[7].v(0, [(8, 16), (1, 8)]), prm.v(PO_DTB, [(0, 16), (1, 8)]), ALU.add,
           [ps[7], prm], [dtb])
        act(dtb.v(), dtb.v(), AF.Exp, [dtb], [dtb])
        act(dtb.v(), dtb.v(), AF.Ln, [dtb], [dtb], bias=1.0)
        tt(adt.v(0, [(8, 16), (1, 8)]), dtb.v(0, [(8, 16), (1, 8)]), Ahat.v(0, [(0, 16), (1, 8)]), ALU.mult, [dtb, Ahat], [adt])
        stage_end(2.2)
        for hg in range(16, 20):
            W, wi = wnext(hg)
            for cc in range(2):
                c = 2 * (hg - 16) + cc
                for G in range(4):
                    if c >= 6 and G == 0:
                        continue
                    pb = nextps()
                    if c >= 6 and G == 1:
                        for kc in range(16):
                            mm(pb.v(0, [(1, 128)]), W.v(kc * 256 + cc * 128, [(1, 128)]), hT.v(kc * 2048 + 896, [(1, 128)]),
                               kc == 0, kc == 15, [W] + hT.k(7), [pb])
                        cp(raw.v(RP + 896, [(1, 128)]), pb.v(0, [(1, 128)]), [pb], rawk(1), eng="act")
                        continue
                    for kc in range(16):
                        mm(pb.v(), W.v(kc * 256 + cc * 128, [(1, 128)]), hT.v(kc * 2048 + G * 512, [(1, 512)]),
                           kc == 0, kc == 15, [W] + hT.k(4 * G, 4 * G + 1, 4 * G + 2, 4 * G + 3), [pb])
                    cp(raw.v(RP + G * 512, [(1, 512)]), pb.v(), [pb], rawk(G), eng="act")
                    a = accs[G % 2]
                    base = RP + G * 512 - 3
                    ts(a.v(), raw.v(base, [(1, 512)]), prm.v(PO_CW + c * 4, [(1, 1)]), prm.v(PO_CB + c, [(1, 1)]),
                       ALU.mult, ALU.add, rawh(G) + [prm], [a])
                    for k in range(1, 4):
                        stt(a.v(), raw.v(base + k, [(1, 512)]), prm.v(PO_CW + c * 4 + k, [(1, 1)]), a.v(),
                            ALU.mult, ALU.add, rawh(G) + [prm, a], [a])
                    act(xbcF.v(c * 2048 + G * 512, [(1, 512)]), a.v(), AF.Silu, [a], [xbcF])
                if cc == 1:
                    wdone(wi)
        stage_end(2.5)
        Wz2 = [wnext(20), wnext(21)]
        Wz = [w_[0] for w_ in Wz2]
        Sprev = [aview(f"sprev{j}", [128, 512], BF16, 32768 + j * 1024, 1024) for j in range(8)]
        acsA = aview("acsA", [128, 256], F32, 40960, 1024)
        Xd2 = aview("Xd2", [128, 512], BF16, 54272, 1024)
        BTb2 = aview("BTb2", [128, 256], BF16, 55296, 512)
        xsT2 = aview("xsT2", [128, 512], BF16, 56320, 1024)
        sc8b = [aview(f"sc8b{i}", [128, 8], F32, 57344 + i * 1024, 32) for i in range(2)]
        XDs, BTs, xsTs = [Xd, Xd2], [BTb, BTb2], [xsT, xsT2]
        abc4 = aview("abc4", [128, 512], F32, 44032, 2048)
        tmpmAB = [aview("tmpmA", [128, 512], F32, 50176, 2048), aview("tmpmB", [128, 512], F32, 52224, 2048)]
        Gs = [Xd, Xd2]
        TMP8s, DTEs, XDTs, CDs = [sc8[0], sc8[4]], [sc8[1], sc8[5]], [sc8[2], sc8b[0]], [sc8[3], sc8b[1]]
        NACs = [Buf(P, "nac0", [128, 8], F32), Buf(P, "nac1", [128, 8], F32)]
        EACSs = [Buf(P, "eacs0", [128, 8], F32), Buf(P, "eacs1", [128, 8], F32)]

        for t in range(16):
            p = t % 2
            psA, psT, psSt = (ps[0], ps[1], ps[2]) if p == 0 else (ps[3], ps[4], ps[5])
            TMP8, DTE, XDT, CD, Xd_, BT_ = TMP8s[p], DTEs[p], XDTs[p], CDs[p], XDs[p], BTs[p]
            ac = acsA.v(t * 16, [(1, 8)])
            tot = acsA.v(t * 16 + 8, [(1, 8)])
            akey = [acsA]
            mm(psA.v(0, [(1, 8)]), cst.v(C_TRI, [(1, 128)]), adt.v(t * 8, [(1, 8)]), True, True, [cst, adt], [psA])
            mm(psA.v(8, [(1, 8)]), cst.v(C_ONE, [(1, 128)]), adt.v(t * 8, [(1, 8)]), True, True, [cst, adt], [psA])
            cp(acsA.v(t * 16, [(1, 16)]), psA.v(0, [(1, 16)]), [psA], akey)
            tt(TMP8.v(), tot, ac, ALU.subtract, akey, [TMP8])
            act(DTE.v(), TMP8.v(), AF.Exp, [TMP8], [DTE])
            tt(XDT.v(), dtb.v(t * 8, [(1, 8)]), DTE.v(), ALU.mult, [dtb, DTE], [XDT])
            act(CD.v(), tot, AF.Exp, akey, [CD])
            for i in range(6):
                tr(psT.vb(i * 128, [(1, 128)]), xbcF.v(i * 2048 + t * 128, [(1, 128)]), identb.v(), [xbcF, identb], [psT])
            tt(Xd_.v(0, [(64, 8), (1, 64)]), psT.vb(0, [(64, 8), (1, 64)]), XDT.v(0, [(1, 8), (0, 64)]), ALU.mult,
               [psT, XDT], [Xd_])
            cp(BT_.v(), psT.vb(512, [(1, 256)]), [psT], [BT_], eng="act")
            if t < 15:
                for g in range(2):
                    mm(psSt.v(g * 256, [(1, 256)]), BT_.v(g * 128, [(1, 128)]), Xd_.v(g * 256, [(1, 256)]), True, True,
                       [BT_, Xd_], [psSt])
                tt(Sst.v(0, [(64, 8), (1, 64)]), Sst.v(0, [(64, 8), (1, 64)]), CD.v(0, [(1, 8), (0, 64)]), ALU.mult,
                   [Sst, CD], [Sst])
                tt(Sst.v(), Sst.v(), psSt.v(), ALU.add, [Sst, psSt], [Sst])
                if t == 7:
                    ts(Sst.v(), Sst.v(), flag.v(), None, ALU.mult, None, [Sst, flag], [Sst])
                if t >= 7:
                    cp(Sprev[t - 7].v(), Sst.v(), [Sst], [Sprev[t - 7]], eng="act")

        zss = [zs, Buf(P, "zs2", [128, 512], BF16)]

        def y_front(j):
            t = 8 + j
            p = j % 2
            xsT_, NAC, EACS = xsTs[p], NACs[p], EACSs[p]
            ac = acsA.v(t * 16, [(1, 8)])
            for i in range(4):
                tr(ps[1].vb(i * 128, [(1, 128)]), xbcF.v(i * 2048 + t * 128, [(1, 128)]), identb.v(), [xbcF, identb], [ps[1]])
            cp(xsT_.v(), ps[1].vb(0, [(1, 512)]), [ps[1]], [xsT_], eng="act")
            tt(Xb.v(0, [(64, 8), (1, 64)]), ps[1].vb(0, [(64, 8), (1, 64)]), dtb.v(t * 8, [(1, 8), (0, 64)]), ALU.mult,
               [ps[1], dtb], [Xb])
            ts(NAC.v(), ac, -1.0, None, ALU.mult, None, [acsA], [NAC])
            act(EACS.v(), ac, AF.Exp, [acsA], [EACS])
            for g in range(2):
                mm(ps[0].v(128 + g * 128, [(1, 128)]), xbcF.v((4 + g) * 2048 + t * 128, [(1, 128)]),
                   xbcF.v((6 + g) * 2048 + t * 128, [(1, 128)]), True, True, [xbcF], [ps[0]])
            v4 = [(128, 4), (1, 128)]
            for b in range(2):
                for hh in range(4):
                    mm(ps[3 + b].v(hh * 128, [(1, 128)]), adt.v(t * 8 + 4 * b + hh, [(0, 128)]), cst.v(C_TRI, [(1, 128)]),
                       True, True, [adt, cst], [ps[3 + b]])
            for zi in range(2):
                for kc in range(16):
                    mm(ps[7].v(zi * 256, [(1, 256)]), hT.v(kc * 2048 + t * 128, [(1, 128)]), Wz[zi].v(kc * 256, [(1, 256)]),
                       kc == 0, kc == 15, [Wz[zi]] + hT.k(t), [ps[7]])
            for b in range(2):
                tt(tmpmAB[b].v(0, v4), ps[3 + b].v(0, v4), cst.v(C_NEG, [(0, 4), (1, 128)]), ALU.add, [ps[3 + b], cst],
                   [tmpmAB[b]])
            for b in range(2):
                tm_ = tmpmAB[b]
                for hh in range(4):
                    act(tm_.v(hh * 128, [(1, 128)]), tm_.v(hh * 128, [(1, 128)]), AF.Exp, [tm_, NAC], [tm_],
                        bias=NAC.v(4 * b + hh, [(1, 1)]))
            for b in range(2):
                tt(Gs[b].v(0, v4), ps[0].v(128 + b * 128, [(0, 4), (1, 128)]), tmpmAB[b].v(0, v4), ALU.mult,
                   [ps[0], tmpmAB[b]], [Gs[b]])
            for b in range(2):
                for hh in range(4):
                    h = 4 * b + hh
                    mm(ps[5].v(h * 64, [(1, 64)]), Gs[b].v(hh * 128, [(1, 128)]), Xb.v(h * 64, [(1, 64)]), True, True,
                       [Gs[b], Xb], [ps[5]])
            for g in range(2):
                mm(ps[6].v(g * 256, [(1, 256)]), xbcF.v((6 + g) * 2048 + t * 128, [(1, 128)]),
                   Sprev[j].v(g * 256, [(1, 256)]), True, True, [xbcF, Sprev[j]], [ps[6]])

        y1s = [ytm[0], accs[0]]

        def y_tail_a(j):
            p = j % 2
            xsT_, EACS = xsTs[p], EACSs[p]
            y1, y2 = y1s[p], ytm[1]
            tt(y1.v(0, [(64, 8), (1, 64)]), ps[6].v(0, [(64, 8), (1, 64)]), EACS.v(0, [(1, 8), (0, 64)]), ALU.mult,
               [ps[6], EACS], [y1])
            tt(y2.v(0, [(64, 8), (1, 64)]), xsT_.v(0, [(64, 8), (1, 64)]), prm.v(PO_DSK, [(1, 8), (0, 64)]), ALU.mult,
               [xsT_, prm], [y2])
            tt(y1.v(), y1.v(), y2.v(), ALU.add, [y1, y2], [y1])
            tt(y1.v(), ps[5].v(), y1.v(), ALU.add, [ps[5], y1], [y1])
            act(y2.v(), ps[7].v(), AF.Exp, [ps[7]], [y2], scale=-1.0)
            act(y2.v(), y2.v(), AF.Ln, [y2], [y2], bias=1.0)
            act(y2.v(), y2.v(), AF.Exp, [y2], [y2], scale=-1.0)
            tt(zss[p].v(), ps[7].v(), y2.v(), ALU.mult, [ps[7], y2], [zss[p]])

        def y_tail_b(j):
            t = 8 + j
            p = j % 2
            y1, y2 = y1s[p], ytm[1]
            tt(y1.v(), y1.v(), zss[p].v(), ALU.mult, [y1, zss[p]], [y1])
            for g in range(2):
                c = 18 + g
                act(y2.v(g * 256, [(1, 256)]), y1.v(g * 256, [(1, 256)]), AF.Square, [y1], [y2] + ss.k(c),
                    accum=ss.v(c, [(1, 1)]))
                act(lnv.v(c, [(1, 1)]), ss.v(c, [(1, 1)]), AF.Ln, ss.k(c), lnv.k(c), scale=1.0 / 256, bias=EPS)
                act(rstd.v(c, [(1, 1)]), lnv.v(c, [(1, 1)]), AF.Exp, lnv.k(c), rstd.k(c), scale=-0.5)
                stt(sob.v(g * 256, [(1, 256)]), y1.v(g * 256, [(1, 256)]), rstd.v(c, [(1, 1)]),
                    prm.v(PO_SSDW + g * 256, [(1, 256)]), ALU.mult, ALU.mult, [y1, prm] + rstd.k(c), [sob])
            for i in range(4):
                tr(ps[2].vb(i * 128, [(1, 128)]), sob.v(i * 128, [(1, 128)]), identb.v(), [sob, identb], [ps[2]])
            cp(mixT.v(8 * 1024 + j * 128, [(1024, 4), (1, 128)]), ps[2].vb(0, [(128, 4), (1, 128)]), [ps[2]],
               [mixT.k((8 + i) * 8 + j)[0] for i in range(4)], eng="act")

        y_front(0)
        y_tail_a(0)
        for j in range(8):
            if j + 1 < 8:
                y_front(j + 1)
            y_tail_b(j)
            if j + 1 < 8:
                y_tail_a(j + 1)

        stage_end(3)
        wdone(Wz2[0][1])
        wdone(Wz2[1][1])
        for b in range(2):
            memset(QT[b].v(), 0.0, [QT[b]])
            memset(VA[b].v(), 1.0, [VA[b]])
            cp(VA[b].v(128, [(130, 8)]), flag.v(0, [(0, 8)]), [flag], [VA[b]])

        pcnt = {"p": 0}
        RC0, RC1, NL, SSQ, LNQ, RSQ = 8, 10, 12, 14, 16, 18

        carry = []
        gn = {"n": 0}

        def attention_gen(nmap, n_kb_base, causal, scale, Kfn, Vfn, fin_head, fin_tail):
            its = []
            for G in range(4):
                nkb = (n_kb_base + 2 * G + 2) if causal else n_kb_base
                for kb in range(nkb):
                    its.append((G, kb, nkb))
            pts = {}
            deferred = carry

            def geom(G, kb):
                i = kb - (n_kb_base + 2 * G) if causal else -1
                q0 = 128 if i >= 1 else 0
                return i, q0, 256 - q0

            def qk(n):
                G, kb, nkb = its[n]
                i, q0, nq = geom(G, kb)
                pcnt["p"] += 1
                pt = PT[pcnt["p"] % 4]
                pts[n] = pt
                pS = ps[4 + n % 2]

                def addmask(c):
                    mm(pS.v(c * 256 + i * 128, [(1, 128)]), identb.v(), negmb.v(), False, True, [identb, negmb], [pS],
                       skip=True)

                if nmap == 2 and q0 > 0:
                    for c in range(2):
                        lhsT, rhs, RR = Kfn(kb, G, q0, nq, c)
                        mm(pS.v(c * 256 + q0, [(1, nq)]), lhsT, rhs, True, False, RR, [pS], skip=True)
                        addmask(c)
                else:
                    lhsT, rhs, RR = Kfn(kb, G, q0, nq)
                    mm(pS.v(q0, [(1, nmap * 256 - q0)]), lhsT, rhs, True, i < 0, RR, [pS], skip=(i >= 0))
                    if i >= 0:
                        for c in range(nmap):
                            addmask(c)
                act(pt.v(q0, [(256, nmap), (1, nq)]), pS.v(q0, [(256, nmap), (1, nq)]), AF.Exp, [pS], [pt], scale=scale)

            qk(0)
            for n in range(len(its)):
                G, kb, nkb = its[n]
                i, q0, nq = geom(G, kb)
                if n + 1 < len(its):
                    qk(n + 1)
                yield
                gn["n"] += 1
                for ent in list(deferred):
                    if ent[0] <= gn["n"]:
                        ent[1]()
                        deferred.remove(ent)
                pt = pts.pop(n)
                for c in range(nmap):
                    for jj in range(2):
                        if jj * 128 < q0:
                            continue
                        last = (n_kb_base + 2 * G + jj) if causal else (nkb - 1)
                        va, RV = Vfn(kb)
                        mm(ps[6 + c].v(jj * 129, [(1, 129)]), pt.v(c * 256 + jj * 128, [(1, 128)]), va,
                           kb == 0 and jj == 0, kb == last, [pt] + RV, [ps[6 + c]], skip=True)
                if kb == nkb - 1:
                    for ent in deferred:
                        ent[1]()
                    del deferred[:]
                    part2 = fin_head(G)
                    dl = 0
                    if part2 is not None:
                        deferred.append((gn["n"] + 3, part2))
                        dl = 3
                    for jj in range(2):
                        deferred.append((gn["n"] + 6 + dl, lambda G=G, jj=jj: fin_tail(G, jj)))

        sgc = {"n": 0}

        def silu_exp(dst_ap, dstbuf, src_ap, srcbuf):
            sgc["n"] += 1
            g = sg[sgc["n"] % 2]
            act(g.v(), src_ap, AF.Exp, [srcbuf], [g], scale=-1.0)
            ts(g.v(), g.v(), 1.0, None, ALU.add, None, [g], [g])
            recip(g.v(), g.v(), [g], [g])
            tt(dst_ap, src_ap, g.v(), ALU.mult, [srcbuf, g], [dstbuf])

        def run_gen(g):
            for _ in g:
                pass

        def diff_proj_gen(h):
            hb = h % 2
            W0, wi0 = wnext(2 * h)
            W1, wi1 = wnext(2 * h + 1)
            sl0, sl1 = wi0 % 3, wi1 % 3
            kt, vaB, qt, gt = KT[hb], VA[hb], QT[hb], GT[hb]
            pend1, pend2 = [], []

            def wrhs(kc, ncol):
                return bass.AP(wbig, sl0 * 4096 + kc * 256, [[3 * 4096, 128], [(sl1 - sl0) * 4096, 2], [1, ncol]])

            def evac1(t, pA):
                own = t >= 8
                j = t - 8
                kr01 = kr01s[t % 2]
                ng = 8 if own else 4
                nk = 256 if own else 128
                if own:
                    cp(vaB.v(t * 130, [(1, 128)]), pA.v(nk, [(1, 128)]), [pA], [vaB], eng="act")
                else:
                    ts(vaB.v(t * 130, [(1, 128)]), pA.v(nk, [(1, 128)]), flag.v(), None, ALU.mult, None, [pA, flag], [vaB])
                if own:
                    sgc["n"] += 1
                    gbuf = sg[sgc["n"] % 2]
                    act(gbuf.v(), pA.v(384, [(1, 128)]), AF.Exp, [pA], [gbuf], scale=-1.0)
                src = pA.v(0, [(32, ng), (1, 32)])
                tt(ropeA.v(0, [(32, ng), (1, 32)]), src, cosb.v(t * 32, [(0, ng), (1, 32)]), ALU.mult, [pA, cosb], [ropeA])
                tt(ropeB.v(0, [(32, ng), (1, 32)]), src, sinb.v(t * 32, [(0, ng), (1, 32)]), ALU.mult, [pA, sinb], [ropeB])
                tt(kr01.v(0, [(64, ng // 2), (1, 32)]), ropeA.v(0, [(64, ng // 2), (1, 32)]),
                   ropeB.v(32, [(64, ng // 2), (1, 32)]), ALU.subtract, [ropeA, ropeB], [kr01])
                tt(kr01.v(32, [(64, ng // 2), (1, 32)]), ropeA.v(32, [(64, ng // 2), (1, 32)]),
                   ropeB.v(0, [(64, ng // 2), (1, 32)]), ALU.add, [ropeA, ropeB], [kr01])
                if own:
                    ts(gbuf.v(), gbuf.v(), 1.0, None, ALU.add, None, [gbuf], [gbuf])
                    recip(gbuf.v(), gbuf.v(), [gbuf], [gbuf])
                    tt(gt.v(j * 128, [(1, 128)]), pA.v(384, [(1, 128)]), gbuf.v(), ALU.mult, [pA, gbuf], [gt])

            def evac2(t):
                own = t >= 8
                j = t - 8
                kr01 = kr01s[t % 2]
                tr(ps[3].vb(0, [(1, 128)]), kr01.v(0, [(1, 128)]), identb.v(), [kr01, identb], [ps[3]])
                if own:
                    tr(ps[3].vb(128, [(1, 128)]), kr01.v(128, [(1, 128)]), identb.v(), [kr01, identb], [ps[3]])
                cp(kt.v(t * 128, [(1, 128)]), ps[3].vb(0, [(1, 128)]), [ps[3]], [kt], eng="act")
                if own:
                    G_, jj_ = j // 2, j % 2
                    for c in range(2):
                        cp(qt.v(G_ * 512 + c * 256 + jj_ * 128, [(1, 128)], p0=c * 64, np_=64),
                           bass.AP(ps[3].vb(128, [(1, 128)]).tensor, c * 64 * 1024 + 128, [[1024, 64], [1, 128]]),
                           [ps[3]], [qt], eng="act")

            pend3 = []

            def flush():
                nonlocal pend1, pend2, pend3
                for f in pend3:
                    f()
                pend3 = pend2
                pend2 = []
                for (tt_, pa_) in pend1:
                    evac1(tt_, pa_)
                    pend2.append(lambda tt_=tt_: evac2(tt_))
                pend1 = []

            for t in range(16):
                own = t >= 8
                pA = ps[t % 3]
                ncol = 256 if own else 128
                nstep = 4 if own else 2
                per = 16 // nstep
                for st in range(nstep):
                    for kc in range(st * per, st * per + per):
                        mm(pA.v(0, [(ncol, 2), (1, ncol)]), hT.v(kc * 2048 + t * 128, [(1, 128)]), wrhs(kc, ncol),
                           kc == 0, kc == 15, [W0, W1] + hT.k(t), [pA])
                    if st == 0:
                        flush()
                    yield
                pend1.append((t, pA))
            wdone(wi0)
            wdone(wi1)
            flush()
            flush()
            flush()

        def diff_attn_gen(h):
            hb = h % 2
            kt, vaB, qt, gt = KT[hb], VA[hb], QT[hb], GT[hb]

            def Kfn(kb, G, q0, nq, c=None):
                if c is not None:
                    return (kt.v(kb * 128, [(1, 128)]), qt.v(G * 512 + c * 256 + q0, [(1, nq)]), [kt, qt])
                return (kt.v(kb * 128, [(1, 128)]), qt.v(G * 512, [(1, 512)]), [kt, qt])

            def Vfn(kb):
                return vaB.v(kb * 130, [(1, 129)]), [vaB]

            def fin_head(G):
                cp(Oev[0].v(), ps[6].v(0, [(1, 258)]), [ps[6]], [Oev[0]], eng="act")
                cp(Oev[1].v(), ps[7].v(0, [(1, 258)]), [ps[7]], [Oev[1]], eng="act")
                F0, F1, F2 = fin
                v3 = [(128, 2), (1, 128)]
                recip(sm.v(RC0, [(1, 2)]), Oev[0].v(128, [(129, 2)]), [Oev[0]], sm.k(RC0, RC0 + 1))
                recip(sm.v(RC1, [(1, 2)]), Oev[1].v(128, [(129, 2)]), [Oev[1]], sm.k(RC1, RC1 + 1))
                tt(sm.v(NL, [(1, 2)]), sm.v(RC1, [(1, 2)]), sm.v(NLAM, [(0, 2)]), ALU.mult,
                   sm.k(RC1, RC1 + 1, NLAM), sm.k(NL, NL + 1))
                tt(F0.v(0, v3), Oev[1].v(0, [(129, 2), (1, 128)]), sm.v(NL, [(1, 2), (0, 128)]), ALU.mult,
                   [Oev[1]] + sm.k(NL, NL + 1), [F0])
                tt(F1.v(0, v3), Oev[0].v(0, [(129, 2), (1, 128)]), sm.v(RC0, [(1, 2), (0, 128)]), ALU.mult,
                   [Oev[0]] + sm.k(RC0, RC0 + 1), [F1])
                tt(F1.v(), F1.v(), F0.v(), ALU.add, [F1, F0], [F1])
                for jj in range(2):
                    P.op("dve", lambda h_, o=F0.v(jj * 128, [(1, 128)]), a=F1.v(jj * 128, [(1, 128)]),
                         acc=sm.v(SSQ + jj, [(1, 1)]): h_.scalar_tensor_tensor(
                        out=o, in0=a, scalar=1.0, in1=a, op0=ALU.mult, op1=ALU.mult, accum_out=acc),
                        [F1], [F0] + sm.k(SSQ + jj))
                return lambda: fin_head2(G)

            def fin_head2(G):
                F0, F1, F2 = fin
                v3 = [(128, 2), (1, 128)]
                act(sm.v(LNQ, [(1, 2)]), sm.v(SSQ, [(1, 2)]), AF.Ln, sm.k(SSQ, SSQ + 1), sm.k(LNQ, LNQ + 1),
                    scale=1.0 / 128, bias=EPS)
                act(sm.v(RSQ, [(1, 2)]), sm.v(LNQ, [(1, 2)]), AF.Exp, sm.k(LNQ, LNQ + 1), sm.k(RSQ, RSQ + 1),
                    scale=-0.5, bias=float(np.log(1.0 - LAM_INIT)))
                tt(F2.v(0, v3), F1.v(0, v3), sm.v(RSQ, [(1, 2), (0, 128)]), ALU.mult, [F1] + sm.k(RSQ, RSQ + 1), [F2])
                tt(F2.v(), F2.v(), gt.v(2 * G * 128, [(1, 256)]), ALU.mult, [F2, gt], [F2])
                tt(dout01.v(0, v3), F2.v(0, v3), prm.v(PO_SUBW, [(0, 2), (1, 128)]), ALU.mult, [F2, prm], [dout01])

            def fin_tail(G, jj):
                if jj == 1:
                    return
                for j2 in range(2):
                    tr(ps[3].vb(512 + j2 * 128, [(1, 128)]), dout01.v(j2 * 128, [(1, 128)]), identb.v(), [dout01, identb],
                       [ps[3]])
                cp(mixT.v(h * 1024 + 2 * G * 128, [(1, 256)]), ps[3].vb(512, [(1, 256)]), [ps[3]],
                   mixT.k(h * 8 + 2 * G, h * 8 + 2 * G + 1), eng="act")

            return attention_gen(2, 8, True, 0.125, Kfn, Vfn, fin_head, fin_tail)

        def xattn_proj_gen(hp):
            Wq, wiq = wnext(22 + hp)
            Wg, wig = wnext(24 + hp)
            sl0, sl1 = wiq % 3, wig % 3
            pend = []

            def wrhs(kc):
                return bass.AP(wbig, sl0 * 4096 + kc * 256, [[3 * 4096, 128], [(sl1 - sl0) * 4096, 2], [1, 256]])

            def evac(j, pA):
                cp(qx.v(), pA.v(0, [(1, 256)]), [pA], [qx], eng="act")
                for hd in range(2):
                    tr(ps[3].vb(hd * 128, [(1, 128)]), qx.v(hd * 128, [(1, 128)]), identb.v(), [qx, identb], [ps[3]])
                    cp(XQ[hp][hd].v(j * 128, [(1, 128)]), ps[3].vb(hd * 128, [(1, 128)]), [ps[3]], [XQ[hp][hd]], eng="act")
                    silu_exp(XG[hp][hd].v(j * 128, [(1, 128)]), XG[hp][hd], pA.v(256 + hd * 128, [(1, 128)]), pA)

            for j in range(8):
                t = 8 + j
                pA = ps[j % 3]
                for st in range(4):
                    for kc in range(st * 4, st * 4 + 4):
                        mm(pA.v(0, [(256, 2), (1, 256)]), hT.v(kc * 2048 + t * 128, [(1, 128)]), wrhs(kc),
                           kc == 0, kc == 15, [Wq, Wg] + hT.k(t), [pA])
                    if st == 0:
                        for (j_, p_) in pend:
                            evac(j_, p_)
                        pend = []
                    yield
                pend.append((j, pA))
            wdone(wiq)
            wdone(wig)
            for (j_, p_) in pend:
                evac(j_, p_)

        def xattn_attn_gen(hp, hd):
            hx = 2 * hp + hd
            xq, xg = XQ[hp][hd], XG[hp][hd]

            def KfnX(kb, G, q0, nq, c=None):
                return (mkT.v(hx * 256 + kb * 128, [(1, 128)]), xq.v(G * 256, [(1, 256)]), [mkT, xq])

            def VfnX(kb):
                return mvA.v((kb * 4 + hx) * 130, [(1, 129)]), [mvA]

            def fin_head_x(G):
                cp(Oev[0].v(), ps[6].v(0, [(1, 258)]), [ps[6]], [Oev[0]], eng="act")
                for jj in range(2):
                    j = 2 * G + jj
                    recip(sm.v(RC0 + jj, [(1, 1)]), Oev[0].v(jj * 129 + 128, [(1, 1)]), [Oev[0]], sm.k(RC0 + jj))
                    stt(dout01.v(jj * 128, [(1, 128)]), Oev[0].v(jj * 129, [(1, 128)]), sm.v(RC0 + jj, [(1, 1)]),
                        xg.v(j * 128, [(1, 128)]), ALU.mult, ALU.mult, [Oev[0], xg] + sm.k(RC0 + jj), [dout01])

            def fin_tail_x(G, jj):
                if jj == 1:
                    return
                for j2 in range(2):
                    tr(ps[3].vb(512 + j2 * 128, [(1, 128)]), dout01.v(j2 * 128, [(1, 128)]), identb.v(), [dout01, identb],
                       [ps[3]])
                cp(mixT.v((12 + hx) * 1024 + 2 * G * 128, [(1, 256)]), ps[3].vb(512, [(1, 256)]), [ps[3]],
                   mixT.k((12 + hx) * 8 + 2 * G, (12 + hx) * 8 + 2 * G + 1), eng="act")

            return attention_gen(1, 2, False, float(128 ** -0.5), KfnX, VfnX, fin_head_x, fin_tail_x)

        def run_with_filler(A, B):
            for _ in A:
                if B is not None:
                    try:
                        next(B)
                    except StopIteration:
                        B = None
            return B

        run_gen(diff_proj_gen(0))
        stage_end(3.3)
        for h in range(8):
            B = diff_proj_gen(h + 1) if h < 7 else xattn_proj_gen(0)
            B = run_with_filler(diff_attn_gen(h), B)
            if B is not None:
                run_gen(B)
            if h == 0:
                stage_end(3.6)

        stage_end(4)
        B = xattn_proj_gen(1)
        for hd in range(2):
            B = run_with_filler(xattn_attn_gen(0, hd), B)
        if B is not None:
            run_gen(B)
        for hd in range(2):
            run_gen(xattn_attn_gen(1, hd))
        for ent in carry:
            ent[1]()
        del carry[:]

        stage_end(5)
        dma(bigrow.v(), rows_d[2:3, :].partition_broadcast(128), [], [bigrow])
        for n in range(8):
            W, wi = wnext(30 + n)
            for j in range(8):
                pb = ps[(n * 8 + j) % 4]
                for fc in range(16):
                    mm(pb.v(0, [(1, 256)]), mixT.v(fc * 1024 + j * 128, [(1, 128)]), W.v(fc * 256, [(1, 256)]),
                       fc == 0, fc == 15, [W] + mixT.k(fc * 8 + j), [pb])
                cp(YB[j].v(n * 256, [(1, 256)]), pb.v(0, [(1, 256)]), [pb], [YB[j]], eng=("act" if j % 2 else "dve"))
            wdone(wi)
        def fstats(j):
            xs = XS[j % 3]
            c = 20 + j
            dma(xs.v(), x_d[(8 + j) * 128:(9 + j) * 128, :], [], [xs])
            act(junk.v(), YB[j].v(), AF.Square, [YB[j]], [junk] + ss.k(c), accum=ss.v(c, [(1, 1)]))
            act(lnv.v(c, [(1, 1)]), ss.v(c, [(1, 1)]), AF.Ln, ss.k(c), lnv.k(c), scale=1.0 / 2048, bias=EPS)
            act(rstd.v(c, [(1, 1)]), lnv.v(c, [(1, 1)]), AF.Exp, lnv.k(c), rstd.k(c), scale=-0.5)

        def frest(j):
            xs = XS[j % 3]
            c = 20 + j
            stt(YB[j].v(), YB[j].v(), rstd.v(c, [(1, 1)]), bigrow.v(), ALU.mult, ALU.mult, [YB[j], bigrow] + rstd.k(c), [YB[j]])
            tt(xs.v(), xs.v(), YB[j].v(), ALU.add, [xs, YB[j]], [xs])
            stores.append(dma(out_d[j * 128:(j + 1) * 128, :], xs.v(), [xs], ["out%d" % j], semkey=xs.name + "_st"))

        fstats(0)
        for j in range(8):
            if j + 1 < 8:
                fstats(j + 1)
            frest(j)
    except _Stop:
        pass
    if dbg:
        stores.append(dma(dbg_d, mixT.v(), [mixT], ["dbgout"], semkey="dbgout"))
        if stage < 4:
            stores.append(dma(dbg2_d, hT.v(), [hT], ["dbgout2"], semkey="dbgout2"))
    cnt = P.emit(final_waits=stores)
    build.names = P.names
    return nc, cnt


def _hg_cols():
    o_q, o_k, o_v, o_g = 0, 1024, 2048, 3072
    o_z, o_xbc, o_dt, o_xq, o_xg = 4096, 4608, 5632, 5640, 6152
    ar = np.arange
    groups = []
    for h in range(8):
        groups.append(("in", np.concatenate([o_k + h * 128 + ar(128), o_q + h * 128 + ar(128)])))
        groups.append(("in", np.concatenate([o_v + h * 128 + ar(128), o_g + h * 128 + ar(128)])))
    for i in range(4):
        groups.append(("in", o_xbc + i * 256 + ar(256)))
    for i in range(2):
        groups.append(("in", o_z + i * 256 + ar(256)))
    for i in range(2):
        groups.append(("in", o_xq + i * 256 + ar(256)))
    for i in range(2):
        groups.append(("in", o_xg + i * 256 + ar(256)))
    for i in range(2):
        groups.append(("kv", i * 256 + ar(256)))
    for i in range(2):
        groups.append(("kv", 512 + i * 256 + ar(256)))
    for i in range(8):
        groups.append(("out", i * 256 + ar(256)))
    return groups, o_dt


_NC_CACHE = {}


def _prep(inputs):
    f32 = np.float32
    w_in = np.asarray(inputs["w_in"], f32)[0]
    w_kv = np.asarray(inputs["w_mem_kv"], f32)[0]
    w_out = np.asarray(inputs["w_out"], f32)[0]
    groups, o_dt = _hg_cols()
    wg = np.empty((NHG, 128, 16, 256), f32)
    src = {"in": w_in, "kv": w_kv, "out": w_out}
    for i, (which, cols) in enumerate(groups):
        wg[i] = src[which][:, cols].reshape(16, 128, 256).transpose(1, 0, 2)
    wg = wg.reshape(NHG, 128, 4096)
    wdt = np.ascontiguousarray(w_in[:, o_dt:o_dt + 8].reshape(16, 128, 8).transpose(1, 0, 2)).reshape(128, 128)
    prm = np.zeros((128, NPRM), f32)
    bc = lambda v: np.broadcast_to(np.asarray(v, f32).reshape(1, -1), (128, np.asarray(v).size))
    prm[:, PO_SSDW:PO_SSDW + 512] = bc(inputs["ssd_norm_w"][0])
    prm[:, PO_SUBW:PO_SUBW + 128] = bc(inputs["diff_subln_w"][0])
    prm[:, PO_DSK:PO_DSK + 8] = bc(inputs["d_skip"][0])
    prm[:, PO_DTB:PO_DTB + 8] = bc(inputs["dt_bias"][0])
    prm[:, PO_ALOG:PO_ALOG + 8] = bc(inputs["a_log"][0])
    for i, nm in enumerate(("lambda_q1", "lambda_k1", "lambda_q2", "lambda_k2")):
        prm[:, PO_LAM + i * 64:PO_LAM + (i + 1) * 64] = bc(inputs[nm][0])
    cw = np.asarray(inputs["conv_w"], f32)[0]
    prm[:, PO_CW:PO_CW + 32] = cw.reshape(4, 8, 128).transpose(2, 1, 0).reshape(128, 32)
    prm[:, PO_CB:PO_CB + 8] = np.asarray(inputs["conv_b"], f32)[0].reshape(8, 128).T
    cf = (1.0 / (10000.0 ** (np.arange(0, 64, 2, dtype=np.float64) / 64.0))) / (2.0 * np.pi)
    c_hi = cf.astype(f32)
    c_lo = (cf - c_hi.astype(np.float64)).astype(f32)
    prm[:, PO_INVF:PO_INVF + 32] = bc(c_hi)
    prm[:, PO_INVL:PO_INVL + 32] = bc(c_lo)
    rows = np.stack([np.asarray(inputs["pre_norm_w"], f32)[0], np.asarray(inputs["mem_norm_w"], f32)[0],
                     np.asarray(inputs["post_norm_w"], f32)[0]])
    cst = np.zeros((128, 512), f32)
    s = np.arange(128)
    cst[:, C_ID:C_ID + 128] = np.eye(128, dtype=f32)
    tri = (s[:, None] <= s[None, :]).astype(f32)
    cst[:, C_TRI:C_TRI + 128] = tri
    cst[:, C_NEG:C_NEG + 128] = (tri - 1.0) * 30000.0
    cst[:, C_ONE:C_ONE + 128] = 1.0
    x = np.asarray(inputs["x"], f32)
    mem = np.asarray(inputs["mem"], f32)
    pos = np.asarray(inputs["positions"], np.int32)
    in_maps = []
    for b in range(4):
        for half in range(2):
            if half == 0:
                xt = np.concatenate([np.zeros((1024, 2048), f32), x[b, :1024]], axis=0)
                pp = np.concatenate([np.zeros(1024, np.int32), pos[b, :1024]])
            else:
                xt = x[b]
                pp = pos[b]
            in_maps.append({
                "x_tok": np.ascontiguousarray(xt), "memx": np.ascontiguousarray(mem[b]),
                "pos": np.ascontiguousarray(pp.reshape(16, 128).T), "flag": np.full((128, 1), float(half), f32),
                "wg": wg, "wdt": wdt, "prm": prm, "rows": rows, "cst": cst})
    return in_maps


def kernel(**inputs):
    if "nc" not in _NC_CACHE:
        _NC_CACHE["nc"] = build()[0]
    nc = _NC_CACHE["nc"]
    in_maps = _prep(inputs)
    res = run_bass_kernel_spmd(nc, in_maps, core_ids=list(range(8)))
    out = np.empty((4, 2048, 2048), np.float32)
    for b in range(4):
        for half in range(2):
            out[b, half * 1024:(half + 1) * 1024] = res.results[2 * b + half]["out"]
    return out
```
